# Optimizing a Trainium2 kernel written in Bass

```python
import jax
import jax.numpy as jnp
from jax import lax
import numpy as np

D_MODEL = 1024
BATCH = 8
SEQ = 2048
DEPTH = 2
DEC_BATCH = 128
DEC_SEQ = 4
PAST_LEN = 2048
PAGE_SIZE = 128

N_HEADS = 16
HEAD_DIM = D_MODEL // N_HEADS
N_KV_HEADS = 4
GROUP = N_HEADS // N_KV_HEADS
KV_WIDTH = N_KV_HEADS * HEAD_DIM
L_CMP = 32
L_SLC = 64
N_SEL = 16
WINDOW = 512
N_BRANCH = 3
CMP_HIDDEN = 2 * HEAD_DIM
C_CONV = D_MODEL // 2
CONV_W = 31
D_FFN = -(-(8 * D_MODEL) // (3 * 256)) * 256
Q_WIDTH = N_HEADS * HEAD_DIM
N_IN = Q_WIDTH + 6 * KV_WIDTH + N_HEADS * N_BRANCH + 2 * C_CONV + 2 * D_MODEL
SPLITS = (Q_WIDTH,
          Q_WIDTH + 6 * KV_WIDTH,
          Q_WIDTH + 6 * KV_WIDTH + N_HEADS * N_BRANCH,
          Q_WIDTH + 6 * KV_WIDTH + N_HEADS * N_BRANCH + 2 * C_CONV)
ATTN_SCALE = HEAD_DIM ** -0.5
SLC_Q_BLOCK = 64
WIN_Q_BLOCK = 128
EPS = 1e-6
FORCE_BONUS = 1e4
NEG_SCORE = -1e9

kernel_name = 'nsa_conformer_hybrid_step'


def rms_norm(x, g):
    x32 = x.astype(jnp.float32)
    y = x32 * lax.rsqrt(jnp.mean(x32 * x32, axis=-1, keepdims=True) + EPS)
    return (y * g.astype(jnp.float32)).astype(x.dtype)


def layer_norm(x, g, b):
    x32 = x.astype(jnp.float32)
    xc = x32 - jnp.mean(x32, axis=-1, keepdims=True)
    var = jnp.mean(xc * xc, axis=-1, keepdims=True)
    y = xc * lax.rsqrt(var + EPS) * g.astype(jnp.float32) + b.astype(jnp.float32)
    return y.astype(x.dtype)


def masked_softmax(s, mask):
    s = jnp.where(mask, s.astype(jnp.float32), -jnp.inf)
    m = jnp.max(s, axis=-1, keepdims=True)
    m = jnp.where(jnp.isfinite(m), m, 0.0)
    p = jnp.exp(s - m)
    d = jnp.sum(p, axis=-1, keepdims=True)
    return p / jnp.where(d > 0, d, 1.0)


def dense_attend(q5, k, v, mask):
    s = jnp.einsum('btgjd,bsgd->btgjs', q5, k) * ATTN_SCALE
    p = masked_softmax(s, mask)
    return jnp.einsum('btgjs,bsgd->btgjd', p.astype(v.dtype), v), p


def window_mask(qpos, kpos):
    d = qpos[:, None] - kpos[None, :]
    return ((d >= 0) & (d < WINDOW) & (kpos[None, :] >= 0))[None, :, None, None, :]


def compress_blocks(rows, w1, b1, w2):
    B, ncb = rows.shape[:2]
    flat = rows.transpose(0, 1, 3, 2, 4).reshape(B, ncb, N_KV_HEADS, L_CMP * HEAD_DIM)
    h = jax.nn.gelu(jnp.einsum('bigf,fh->bigh', flat, w1) + b1)
    return jnp.einsum('bigh,hd->bigd', h, w2)


def _selected_chunk(q5, idx, qpos, k_blk, v_blk):
    B, Tc = q5.shape[:2]
    n_sel = idx.shape[-1]
    b_ix = jnp.arange(B)[:, None, None, None]
    g_ix = jnp.arange(N_KV_HEADS)[None, None, :, None]
    kg = k_blk[b_ix, g_ix, idx].reshape(B, Tc, N_KV_HEADS, n_sel * L_SLC, HEAD_DIM)
    vg = v_blk[b_ix, g_ix, idx].reshape(B, Tc, N_KV_HEADS, n_sel * L_SLC, HEAD_DIM)
    kpos = (idx[..., None] * L_SLC + jnp.arange(L_SLC)).reshape(B, Tc, N_KV_HEADS, n_sel * L_SLC)
    mask = (kpos <= qpos[None, :, None, None])[:, :, :, None, :]
    s = jnp.einsum('btgjd,btgkd->btgjk', q5, kg) * ATTN_SCALE
    p = masked_softmax(s, mask)
    return jnp.einsum('btgjk,btgkd->btgjd', p.astype(vg.dtype), vg)


def selected_attention(q5, idx, qpos, k_blk, v_blk):
    T = q5.shape[1]
    if T <= SLC_Q_BLOCK or T % SLC_Q_BLOCK:
        return _selected_chunk(q5, idx, qpos, k_blk, v_blk)
    nq = T // SLC_Q_BLOCK

    def to_chunks(a):
        return a.reshape(a.shape[0], nq, SLC_Q_BLOCK, *a.shape[2:]).swapaxes(0, 1)

    xs = (to_chunks(q5), to_chunks(idx), qpos.reshape(nq, SLC_Q_BLOCK))
    out = lax.map(lambda t: _selected_chunk(t[0], t[1], t[2], k_blk, v_blk), xs)
    return out.swapaxes(0, 1).reshape(q5.shape)


def nsa_global(q5, kv4, qpos, w_cmp1, b_cmp1, w_cmp2):
    B, S = kv4.shape[:2]
    T = q5.shape[1]
    s_pad = -(-S // L_SLC) * L_SLC
    kv4 = jnp.pad(kv4, ((0, 0), (0, s_pad - S), (0, 0), (0, 0), (0, 0)))
    n_cb, n_sb = s_pad // L_CMP, s_pad // L_SLC
    blocks = kv4.reshape(B, n_cb, L_CMP, 4, N_KV_HEADS, HEAD_DIM)
    k_c = compress_blocks(blocks[:, :, :, 0], w_cmp1[0], b_cmp1[0], w_cmp2[0])
    v_c = compress_blocks(blocks[:, :, :, 1], w_cmp1[1], b_cmp1[1], w_cmp2[1])
    cb_end = (jnp.arange(n_cb) + 1) * L_CMP - 1
    mask_c = (cb_end[None, :] <= qpos[:, None])[None, :, None, None, :]
    o_cmp, p_cmp = dense_attend(q5, k_c, v_c, mask_c)
    p_slc = p_cmp.reshape(B, T, N_KV_HEADS, GROUP, n_sb, L_SLC // L_CMP).sum(axis=(3, 5))
    sb = jnp.arange(n_sb)[None, :]
    cur = (qpos // L_SLC)[:, None]
    forced = (sb == 0) | (sb == cur) | (sb == cur - 1)
    valid = sb * L_SLC <= qpos[:, None]
    score = p_slc + jnp.where(forced, FORCE_BONUS, 0.0)[None, :, None, :]
    score = jnp.where(valid[None, :, None, :], score, NEG_SCORE)
    _, idx = lax.top_k(score, min(N_SEL, n_sb))
    k_blk = kv4[:, :, 2].reshape(B, n_sb, L_SLC, N_KV_HEADS, HEAD_DIM).transpose(0, 3, 1, 2, 4)
    v_blk = kv4[:, :, 3].reshape(B, n_sb, L_SLC, N_KV_HEADS, HEAD_DIM).transpose(0, 3, 1, 2, 4)
    o_slc = selected_attention(q5, idx, qpos, k_blk, v_blk)
    return o_cmp, o_slc


def window_prompt(q5, kw, vw):
    B, T = q5.shape[:2]
    n_prev = -(-WINDOW // WIN_Q_BLOCK)
    pad = n_prev * WIN_Q_BLOCK
    band = pad + WIN_Q_BLOCK
    kp = jnp.pad(kw, ((0, 0), (pad, 0), (0, 0), (0, 0)))
    vp = jnp.pad(vw, ((0, 0), (pad, 0), (0, 0), (0, 0)))
    nb = T // WIN_Q_BLOCK
    qb = q5.reshape(B, nb, WIN_Q_BLOCK, N_KV_HEADS, GROUP, HEAD_DIM).swapaxes(0, 1)

    def one(args):
        q_i, i = args
        start = i * WIN_Q_BLOCK
        k_i = lax.dynamic_slice_in_dim(kp, start, band, axis=1)
        v_i = lax.dynamic_slice_in_dim(vp, start, band, axis=1)
        qpos = start + jnp.arange(WIN_Q_BLOCK)
        kpos = start - pad + jnp.arange(band)
        o, _ = dense_attend(q_i, k_i, v_i, window_mask(qpos, kpos))
        return o

    out = lax.map(one, (qb, jnp.arange(nb)))
    return out.swapaxes(0, 1).reshape(q5.shape)


def split_projection(u, w_in):
    B, T = u.shape[:2]
    z = jnp.einsum('btd,dn->btn', u, w_in)
    q, kv, g, glu, mg = jnp.split(z, SPLITS, axis=-1)
    q5 = q.reshape(B, T, N_KV_HEADS, GROUP, HEAD_DIM)
    kv6 = kv.reshape(B, T, 6, N_KV_HEADS, HEAD_DIM)
    g3 = jax.nn.sigmoid(g.reshape(B, T, N_KV_HEADS, GROUP, N_BRANCH))
    a, b = jnp.split(glu, 2, axis=-1)
    conv_in = a * jax.nn.sigmoid(b)
    ga, gb = jnp.split(jax.nn.sigmoid(mg), 2, axis=-1)
    return q5, kv6, g3, conv_in, ga, gb


def nsa_combine(g3, o_cmp, o_slc, o_win):
    o = g3[..., 0:1] * o_cmp + g3[..., 1:2] * o_slc + g3[..., 2:3] * o_win
    return o.reshape(o.shape[0], o.shape[1], Q_WIDTH)


def conformer_conv(hist, lp):
    y = lax.conv_general_dilated(hist, lp['w_dw'][:, None, :], window_strides=(1,), padding='VALID',
                                 dimension_numbers=('NWC', 'WIO', 'NWC'),
                                 feature_group_count=C_CONV) + lp['b_dw']
    y = jax.nn.silu(layer_norm(y, lp['ln_conv_g'], lp['ln_conv_b']))
    return jnp.einsum('btc,cd->btd', y, lp['w_pw2'])


def merge_out(o_nsa, o_conv, ga, gb, w_out):
    return jnp.einsum('btd,de->bte', ga * o_nsa + gb * o_conv, w_out)


def mixer_prompt(u, lp):
    q5, kv6, g3, conv_in, ga, gb = split_projection(u, lp['w_in'])
    T = u.shape[1]
    qpos = jnp.arange(T)
    kv4 = kv6[:, :, :4]
    o_cmp, o_slc = nsa_global(q5, kv4, qpos, lp['w_cmp1'], lp['b_cmp1'], lp['w_cmp2'])
    o_win = window_prompt(q5, kv6[:, :, 4], kv6[:, :, 5])
    o_nsa = nsa_combine(g3, o_cmp, o_slc, o_win)
    hist = jnp.pad(conv_in, ((0, 0), (CONV_W - 1, 0), (0, 0)))
    o_conv = conformer_conv(hist, lp)
    y = merge_out(o_nsa, o_conv, ga, gb, lp['w_out'])
    wb = min(WINDOW, T)
    return y, (kv4, kv6[:, T - wb:, 4:], hist[:, hist.shape[1] - (CONV_W - 1):])


def mixer_sample(u, lp, cache_l, page_table, win_l, conv_l):
    q5, kv6, g3, conv_in, ga, gb = split_projection(u, lp['w_in'])
    B, T = u.shape[:2]
    past_len = page_table.shape[1] * cache_l.shape[1]
    qpos = past_len + jnp.arange(T)
    past = cache_l[page_table].reshape(B, past_len, 4, N_KV_HEADS, HEAD_DIM)
    kv4_full = jnp.concatenate([past, kv6[:, :, :4]], axis=1)
    o_cmp, o_slc = nsa_global(q5, kv4_full, qpos, lp['w_cmp1'], lp['b_cmp1'], lp['w_cmp2'])
    wb = win_l.shape[1]
    win_full = jnp.concatenate([win_l, kv6[:, :, 4:]], axis=1)
    kpos = past_len - wb + jnp.arange(wb + T)
    o_win, _ = dense_attend(q5, win_full[:, :, 0], win_full[:, :, 1], window_mask(qpos, kpos))
    o_nsa = nsa_combine(g3, o_cmp, o_slc, o_win)
    hist = jnp.concatenate([conv_l, conv_in], axis=1)
    o_conv = conformer_conv(hist, lp)
    y = merge_out(o_nsa, o_conv, ga, gb, lp['w_out'])
    return y, (kv6[:, :, :4], win_full[:, T:], hist[:, T:])


def trunk_layer(x, c, mixer, lp):
    ada = jnp.einsum('bd,dn->bn', jax.nn.silu(c), lp['w_ada']) + lp['b_ada']
    sh_m, sc_m, gt_m, sh_f, sc_f, gt_f = [a[:, None, :] for a in jnp.split(ada, 6, axis=-1)]
    g = lp['norm_gain']
    h = rms_norm(x, g[0]) * (1.0 + sc_m) + sh_m
    y, st = mixer(h)
    x = x + gt_m * rms_norm(y, g[1])
    h = rms_norm(x, g[2]) * (1.0 + sc_f) + sh_f
    a, b = jnp.split(jnp.einsum('btd,df->btf', h, lp['w_ffn_in']), 2, axis=-1)
    y = jnp.einsum('btf,fd->btd', jax.nn.silu(a) * b, lp['w_ffn_out'])
    x = x + gt_f * rms_norm(y, g[3])
    return x, st


def setup_inputs(seed: int = 0) -> dict:
    key = jax.random.key(seed)
    ks = jax.random.split(key, 24)

    def nrm(k, shape, scale=1.0):
        return jax.random.normal(k, shape, jnp.float32) * scale

    n_pages = PAST_LEN // PAGE_SIZE
    n_used = DEC_BATCH * n_pages
    n_pool = n_used + (n_used + 3) // 4
    wb = min(WINDOW, PAST_LEN)
    page_table = jax.random.permutation(ks[5], n_pool)[:n_used].reshape(DEC_BATCH, n_pages).astype(jnp.int32)
    return {
        'x_prompt': nrm(ks[0], (BATCH, SEQ, D_MODEL)),
        'x_sample': nrm(ks[1], (DEC_BATCH, DEC_SEQ, D_MODEL)),
        'cache_kv': nrm(ks[2], (DEPTH, n_pool, PAGE_SIZE, 4, N_KV_HEADS, HEAD_DIM)),
        'state_win': nrm(ks[3], (DEPTH, DEC_BATCH, wb, 2, N_KV_HEADS, HEAD_DIM)),
        'state_conv': nrm(ks[4], (DEPTH, DEC_BATCH, CONV_W - 1, C_CONV), 0.5),
        'page_table': page_table,
        'c_prompt': nrm(ks[6], (BATCH, D_MODEL)),
        'c_sample': nrm(ks[7], (DEC_BATCH, D_MODEL)),
        'w_ada': nrm(ks[8], (DEPTH, D_MODEL, 6 * D_MODEL), 0.5 * D_MODEL ** -0.5),
        'b_ada': nrm(ks[9], (DEPTH, 6 * D_MODEL), 0.01),
        'norm_gain': 1.0 + nrm(ks[10], (DEPTH, 4, D_MODEL), 0.02),
        'w_in': nrm(ks[11], (DEPTH, D_MODEL, N_IN), D_MODEL ** -0.5),
        'w_cmp1': nrm(ks[12], (DEPTH, 2, L_CMP * HEAD_DIM, CMP_HIDDEN), (L_CMP * HEAD_DIM) ** -0.5),
        'b_cmp1': nrm(ks[13], (DEPTH, 2, CMP_HIDDEN), 0.01),
        'w_cmp2': nrm(ks[14], (DEPTH, 2, CMP_HIDDEN, HEAD_DIM), CMP_HIDDEN ** -0.5),
        'w_dw': nrm(ks[15], (DEPTH, CONV_W, C_CONV), CONV_W ** -0.5),
        'b_dw': nrm(ks[16], (DEPTH, C_CONV), 0.01),
        'ln_conv_g': 1.0 + nrm(ks[17], (DEPTH, C_CONV), 0.02),
        'ln_conv_b': nrm(ks[18], (DEPTH, C_CONV), 0.01),
        'w_pw2': nrm(ks[19], (DEPTH, C_CONV, D_MODEL), C_CONV ** -0.5),
        'w_out': nrm(ks[20], (DEPTH, D_MODEL, D_MODEL), D_MODEL ** -0.5),
        'w_ffn_in': nrm(ks[21], (DEPTH, D_MODEL, 2 * D_FFN), D_MODEL ** -0.5),
        'w_ffn_out': nrm(ks[22], (DEPTH, D_FFN, D_MODEL), D_FFN ** -0.5),
    }


def reference(x_prompt, x_sample, cache_kv, state_win, state_conv, page_table, c_prompt, c_sample,
              w_ada, b_ada, norm_gain, w_in, w_cmp1, b_cmp1, w_cmp2, w_dw, b_dw, ln_conv_g,
              ln_conv_b, w_pw2, w_out, w_ffn_in, w_ffn_out):
    y_p, y_s = x_prompt, x_sample
    kv_p, kv_s, win_p, win_s, conv_p, conv_s = [], [], [], [], [], []
    for l in range(DEPTH):
        lp = {'w_ada': w_ada[l], 'b_ada': b_ada[l], 'norm_gain': norm_gain[l], 'w_in': w_in[l],
              'w_cmp1': w_cmp1[l], 'b_cmp1': b_cmp1[l], 'w_cmp2': w_cmp2[l], 'w_dw': w_dw[l],
              'b_dw': b_dw[l], 'ln_conv_g': ln_conv_g[l], 'ln_conv_b': ln_conv_b[l],
              'w_pw2': w_pw2[l], 'w_out': w_out[l], 'w_ffn_in': w_ffn_in[l], 'w_ffn_out': w_ffn_out[l]}
        y_p, st_p = trunk_layer(y_p, c_prompt, lambda u: mixer_prompt(u, lp), lp)
        y_s, st_s = trunk_layer(
            y_s, c_sample,
            lambda u: mixer_sample(u, lp, cache_kv[l], page_table, state_win[l], state_conv[l]), lp)
        kv_p.append(st_p[0]); win_p.append(st_p[1]); conv_p.append(st_p[2])
        kv_s.append(st_s[0]); win_s.append(st_s[1]); conv_s.append(st_s[2])
    kv_prompt = jnp.stack(kv_p)
    kv_sample = jnp.stack(kv_s)
    win_prompt = jnp.stack(win_p)
    win_sample = jnp.stack(win_s)
    conv_prompt = jnp.stack(conv_p)
    conv_sample = jnp.stack(conv_s)
    return (y_p, y_s, kv_prompt, kv_sample, win_prompt, win_sample, conv_prompt, conv_sample)
```

```python
import numpy as np
import ml_dtypes
import concourse.bass as bass
import concourse.mybir as mybir
from concourse.bass_utils import run_bass_kernel_spmd
from contextlib import ExitStack

F32 = mybir.dt.float32
BF16 = mybir.dt.bfloat16
I32 = mybir.dt.int32
AF = mybir.ActivationFunctionType
ALU = mybir.AluOpType

SAME_ENG_SYNC = True

D = 1024
KD = 8
NT = 17
TT = 2112
DEPTH = 2
N_IN = 5680
EPS = 1e-6
ATTN_SCALE = 0.125
NPOOL_ROWS = 2560 * 128
D_FFN = 2816
NFC = 22
STOP = None
NO_SAMPLE = False
DBG = {}


def tcols(j):
    return (j * 128, 128) if j < 16 else (2048, 64)


class Prog:
    ENGS = ("pe", "act", "dve", "pool", "sp")

    def __init__(self, nc):
        self.nc = nc
        self.ops = {e: [] for e in self.ENGS}
        self.cnt = {e: 0 for e in self.ENGS}
        self.waited = {e: {} for e in self.ENGS}
        self.state = {}
        self.dma_cnt = {}
        self.dma_seq = {}
        self.regcache = {}
        self.sems = {}
        self.n_instr = 0

    def _deps(self, reads, writes):
        deps = {}

        def add(sv):
            s, v = sv
            if deps.get(s, 0) < v:
                deps[s] = v

        for k in reads:
            st = self.state.get(k)
            if st is not None and st[0] is not None:
                add(st[0])
        for k in writes:
            st = self.state.get(k)
            if st is not None:
                if st[0] is not None:
                    add(st[0])
                for sv in st[1].items():
                    add(sv)
        return deps

    def _filter(self, eng, deps):
        own = "E:" + eng
        own_cnt = self.cnt[eng]
        w = self.waited[eng]
        out = []
        for s, v in deps.items():
            if s == own:
                if v > own_cnt:
                    continue
                if not SAME_ENG_SYNC or eng == "pe":
                    continue
            if w.get(s, 0) >= v:
                continue
            w[s] = v
            out.append((s, v))
        return out

    def _update(self, reads, writes, sv):
        s, v = sv
        for k in reads:
            st = self.state.setdefault(k, [None, {}])
            if st[1].get(s, 0) < v:
                st[1][s] = v
        for k in writes:
            self.state[k] = [sv, {}]

    def op(self, eng, fn, reads=(), writes=(), signal=True):
        deps = self._deps(reads, writes)
        waits = self._filter(eng, deps)
        sk = "E:" + eng
        idx = self.cnt[eng] + 1
        if signal:
            self.cnt[eng] = idx
        self.ops[eng].append((waits, fn, sk if signal else None, 1))
        self._update(reads, writes, (sk, idx))
        self.n_instr += 1

    NRING = 24

    def dma(self, q, out, in_, reads=(), writes=(), stream="d", indirect=None, **kw):
        i = self.dma_seq.get(q, 0)
        self.dma_seq[q] = i + 1
        sk = "D:%s:%02d" % (q, i % self.NRING)
        prev = self.dma_cnt.get(sk, 0)
        val = prev + 16
        self.dma_cnt[sk] = val
        deps = self._deps(reads, writes)
        if prev > 0:
            if deps.get(sk, 0) < prev:
                deps[sk] = prev
        waits = self._filter(q, deps)
        if indirect is None:
            fn = lambda e: e.dma_start(out=out, in_=in_, **kw)
        else:
            regc = self.regcache

            def fn(e):
                kw2 = dict(kw)
                bc = kw2.get("bounds_check")
                if isinstance(bc, int):
                    if bc not in regc:
                        regc[bc] = e.to_reg(bc)
                    kw2["bounds_check"] = regc[bc]
                return e.indirect_dma_start(out=out, out_offset=None, in_=in_, in_offset=indirect, **kw2)
        self.ops[q].append((waits, fn, sk, 16))
        self._update(reads, writes, (sk, val))
        self.n_instr += 1

    def barrier(self):
        allv = {("E:" + e): c for e, c in self.cnt.items() if c > 0}
        allv.update(self.dma_cnt)
        for e in self.ENGS:
            w = self.waited[e]
            waits = []
            for s, v in allv.items():
                if w.get(s, 0) < v:
                    w[s] = v
                    waits.append((s, v))
            if waits:
                self.ops[e].append((waits, None, None, 0))
        self.state = {}

    def emit(self, stack):
        nc = self.nc
        keys = ["E:" + e for e in self.ENGS] + sorted(self.dma_cnt.keys())
        for i, k in enumerate(keys):
            self.sems[k] = stack.enter_context(nc.semaphore("sm%d" % i))
        block = stack.enter_context(nc.Block())
        regs = {"pe": block.tensor, "act": block.scalar, "dve": block.vector,
                "pool": block.gpsimd, "sp": block.sync}
        sems = self.sems
        for e in self.ENGS:
            ops = self.ops[e]

            def body(engine, ops=ops):
                for waits, fn, sk, inc in ops:
                    for (s, v) in waits:
                        engine.wait_ge(sems[s], v)
                    if fn is not None:
                        ins = fn(engine)
                        if sk is not None:
                            ins.then_inc(sems[sk], inc)

            regs[e](body)


class Lay:
    def __init__(self, arena, start, end):
        self.arena = arena
        self.off = start
        self.end = end

    def take(self, free_shape, dt, parts=128, p0=0):
        n = 1
        for s in free_shape:
            n *= s
        esz = 4 if dt in (F32, I32) else 2
        nbytes = (n * esz + 31) // 32 * 32
        o = self.off
        assert o % 4 == 0
        self.off += nbytes
        assert self.off <= self.end, ("arena overflow", self.off, self.end)
        v = self.arena[p0:p0 + parts, o // 2:(o + n * esz) // 2]
        if esz == 4:
            v = v.bitcast(dt)
        if len(free_shape) == 2:
            v = v.rearrange("p (a b) -> p a b", b=free_shape[1])
        elif len(free_shape) == 3:
            v = v.rearrange("p (a b c) -> p a b c", b=free_shape[1], c=free_shape[2])
        elif len(free_shape) == 4:
            v = v.rearrange("p (a b c d) -> p a b c d", b=free_shape[1], c=free_shape[2], d=free_shape[3])
        return v


def bc_last(ap, n):
    shp = list(ap.shape)
    return ap.unsqueeze(len(shp)).broadcast_to(shp + [n])


def bc_mid(ap, n):
    shp = list(ap.shape)
    return ap.unsqueeze(1).broadcast_to([shp[0], n] + shp[1:])


def build_program(dbg_names=()):
    nc = bass.Bass("TRN2", target_bir_lowering=False)
    P = Prog(nc)
    early = NO_SAMPLE

    def din(name, shape, dt=F32):
        return nc.dram_tensor(name, list(shape), dt, kind="ExternalInput")

    def dout(name, shape, dt=F32):
        return nc.dram_tensor(name, list(shape), dt, kind="ExternalOutput")

    SPECS = {
        "xp": ([2048, D], F32), "xs": ([64, D], F32), "c17": ([17, D], F32),
        "cache": ([DEPTH * NPOOL_ROWS, 1024], F32), "pt": ([16, 16], I32),
        "swin": ([DEPTH, 16, 512, 512], F32), "sconv": ([DEPTH, 480, 512], F32),
        "w_ada": ([DEPTH, D, 6144], F32), "b_ada": ([DEPTH, 6144], F32), "gain": ([DEPTH, 4, D], F32),
        "w_in": ([DEPTH, D, N_IN], F32), "w_cmp1": ([DEPTH, 2, 2048, 128], F32), "b_cmp1": ([DEPTH, 2, 128], F32),
        "w_cmp2": ([DEPTH, 2, 128, 64], F32), "cvp": ([DEPTH, 34, 512], F32), "w_pw2": ([DEPTH, 512, D], F32),
        "w_out": ([DEPTH, D, D], F32), "w_f1": ([DEPTH, D, 2 * D_FFN], F32), "w_f2": ([DEPTH, D_FFN, D], F32),
        "k_identb": ([128, 128], BF16), "k_identf": ([128, 128], F32), "k_trile": ([128, 128], BF16),
        "k_trigt": ([128, 128], BF16), "k_mstx": ([128, 2048], BF16), "k_cmpmask": ([64, 2048], BF16),
        "k_vcac": ([64, 34], BF16), "k_selbias": ([128, 512], F32), "k_selbias_s": ([4, 33], F32),
        "k_selp": ([17, 128], F32), "k_sels": ([17, 64], F32), "k_rs": ([16, 16], F32), "k_rt": ([16, 4], F32),
        "k_wmask": ([128, 4], BF16), "k_onesf": ([128, 128], F32), "k_iota": ([128, 1], I32),
    }
    _INS = {}

    def inp(name):
        if name not in _INS:
            shape, dt = SPECS[name]
            _INS[name] = din(name, shape, dt)
        return _INS[name].ap()

    yp = dout("yp", [2048, D]).ap()
    ys = dout("ys", [64, D]).ap()
    kvp = dout("kvp", [DEPTH, 2048, 1024]).ap()
    kvs = dout("kvs", [DEPTH, 64, 1024]).ap()
    winp = dout("winp", [DEPTH, 512, 512]).ap()
    wins = dout("wins", [DEPTH, 16, 512, 512]).ap()
    convp = dout("convp", [DEPTH, 30, 512]).ap()
    convs = dout("convs", [DEPTH, 16, 30, 512]).ap()
    dbg_out = {}

    with ExitStack() as st:
        arena = st.enter_context(nc.sbuf_tensor("arena", [128, 104000], BF16))
        PSB = [st.enter_context(nc.psum_tensor("ps%d" % i, [128, 512], F32)) for i in range(8)]

        def ps(i):
            return PSB[i][:]

        def psk(i):
            return ("ps", i)

        def mm(out, lhsT, rhs, start, stop, R, W, signal=True, skip=False):
            P.op("pe", lambda e: e.matmul(out, lhsT=lhsT, rhs=rhs, start=start, stop=stop, skip_group_check=skip),
                 reads=R, writes=W, signal=signal)

        def tr(out, in_, ident, R, W, signal=True):
            P.op("pe", lambda e: e.transpose(out=out, in_=in_, identity=ident), reads=R, writes=W, signal=signal)

        def act(out, in_, func, R, W, scale=None, bias=None, accum=None):
            kw = {}
            if scale is not None:
                kw["scale"] = scale
            if bias is not None:
                kw["bias"] = bias
            if accum is not None:
                kw["accum_out"] = accum
            P.op("act", lambda e: e.activation(out=out, in_=in_, func=func, **kw), reads=R, writes=W)

        def tt(eng, out, in0, in1, op, R, W):
            P.op(eng, lambda e: e.tensor_tensor(out=out, in0=in0, in1=in1, op=op), reads=R, writes=W)

        def tsc(eng, out, in0, s1, op0, R, W, s2=None, op1=None):
            if op1 is None:
                P.op(eng, lambda e: e.tensor_scalar(out=out, in0=in0, scalar1=s1, scalar2=None, op0=op0), reads=R, writes=W)
            else:
                P.op(eng, lambda e: e.tensor_scalar(out=out, in0=in0, scalar1=s1, scalar2=s2, op0=op0, op1=op1),
                     reads=R, writes=W)

        def stt(out, in0, scalar, in1, op0, op1, R, W):
            P.op("dve", lambda e: e.scalar_tensor_tensor(out=out, in0=in0, scalar=scalar, in1=in1, op0=op0, op1=op1),
                 reads=R, writes=W)

        def cp(eng, out, in_, R, W):
            if eng == "act":
                P.op("act", lambda e: e.copy(out=out, in_=in_), reads=R, writes=W)
            else:
                P.op(eng, lambda e: e.tensor_copy(out=out, in_=in_), reads=R, writes=W)

        def recip(out, in_, R, W):
            P.op("dve", lambda e: e.reciprocal(out=out, in_=in_), reads=R, writes=W)

        def memset(eng, ap, val, W):
            P.op(eng, lambda e: e.memset(ap, val), writes=W)

        def ld(out, in_, W, R=(), stream="in"):
            P.dma("sp", out, in_, reads=R, writes=W, stream=stream)

        def ldc(out, in_, W, R=(), stream="w"):
            P.dma("pool", out, in_, reads=R, writes=W, stream=stream)

        def stv(out, in_, R, W=(), stream="out"):
            P.dma("sp", out, in_, reads=R, writes=W, stream=stream)

        def dump(name, ap, shape, dt=F32):
            if name in dbg_names:
                t = dout("dbg_" + name, shape, dt).ap()
                dbg_out[name] = t
                P.barrier()
                P.dma("sp", t, ap, stream="out")
                P.barrier()

        LP = Lay(arena, 0, 28672)
        IDB = LP.take([128], BF16)
        IDF = LP.take([128], F32)
        TRILE = LP.take([128], BF16)
        TRIGT = LP.take([128], BF16)
        MSTX = LP.take([2048], BF16)
        CMPMASK = LP.take([2048], BF16, parts=64)
        VCAC = LP.take([34], BF16, parts=64)
        SELBIAS = LP.take([16, 32], F32)
        SELBIAS_S = LP.take([33], F32, parts=4)
        SELP = LP.take([128], F32, parts=17)
        SELS = LP.take([64], F32, parts=17)
        RSM = LP.take([4, 4], F32, parts=16)
        RTM = LP.take([4], F32, parts=16)
        WMASK = LP.take([4], BF16)
        ONESF = LP.take([128], F32)
        IOTA = LP.take([1], I32)
        SCT = LP.take([8, 32], BF16)
        ADAT = LP.take([48, 17], F32)
        GNT = LP.take([8, 4], F32)
        A_M = LP.take([8, 17], F32)
        A_F = LP.take([8, 17], F32)
        ADA_GT = LP.take([2, 1024], F32, parts=17)
        SS = LP.take([8], F32)
        PIDX = LP.take([256], I32)
        HT = Lay(arena, 28672, 62464).take([8, TT], BF16)
        OFF_YC = 62464
        OFF_DYN = 79360
        OFF_END = 208000
        YC = Lay(arena, OFF_YC, OFF_DYN).take([4, TT], BF16)

        for (dst, src, key) in [(IDB, inp("k_identb"), "idb"), (IDF, inp("k_identf"), "idf"), (TRILE, inp("k_trile"), "trile"),
                                (TRIGT, inp("k_trigt"), "trigt"), (MSTX, inp("k_mstx"), "mstx"), (CMPMASK, inp("k_cmpmask"), "cmpmask"),
                                (VCAC, inp("k_vcac"), "vcac"), (SELBIAS, inp("k_selbias").rearrange("p (a b) -> p a b", b=32), "selbias"),
                                (SELBIAS_S, inp("k_selbias_s"), "selbias_s"), (SELP, inp("k_selp"), "selp"), (SELS, inp("k_sels"), "sels"),
                                (RSM, inp("k_rs").rearrange("p (a b) -> p a b", b=4), "rsm"), (RTM, inp("k_rt"), "rtm"),
                                (WMASK, inp("k_wmask"), "wmask"), (ONESF, inp("k_onesf"), "onesf"), (IOTA, inp("k_iota"), "iota")]:
            ld(dst, src, W=[key])
        P.barrier()

        def norm_to_T(L, j, xt, xt_key, A, SH, DST, dst_key, bank, dcol=None):
            c0, n = tcols(j)
            if dcol is not None:
                c0 = dcol
            SQ = L["SQ"]
            XN = L["XN"]
            ssv = SS[0:n, 0:1]
            rsv = SS[0:n, 1:2]
            act(SQ[0:n, :], xt[0:n, :], AF.Square, R=[xt_key], W=["SQ", "ss0"], accum=ssv)
            act(rsv, ssv, AF.Sqrt, R=["ss0"], W=["ss1"], scale=1.0 / D, bias=EPS)
            recip(rsv, rsv, R=["ss1"], W=["ss1"])
            tsc("dve", XN[0:n, :], xt[0:n, :], rsv, ALU.mult, R=[xt_key, "ss1"], W=["XN"])
            pT = ps(bank).bitcast(BF16)
            for k in range(8):
                tr(pT[:, k * 128:k * 128 + n], XN[0:n, k * 128:(k + 1) * 128], IDB[0:n, 0:n], R=["XN", "idb"],
                   W=[psk(bank)], signal=(k == 7))
            if j < 16:
                for k in range(8):
                    if k % 2 == 0:
                        act(DST[:, k, c0:c0 + 128], pT[:, k * 128:(k + 1) * 128], AF.Identity, R=[psk(bank)],
                            W=[(dst_key, j)], scale=A[:, k, 0:1], bias=SH[:, k, 0:1])
                    else:
                        tsc("dve", DST[:, k, c0:c0 + 128], pT[:, k * 128:(k + 1) * 128], A[:, k, 0:1], ALU.mult,
                            R=[psk(bank)], W=[(dst_key, j)], s2=SH[:, k, 0:1], op1=ALU.add)
            else:
                TMPS = L["TMPS"]
                for k in range(8):
                    src = pT[:, k * 128:k * 128 + 64].rearrange("p (b t) -> p b t", t=4)
                    tt("dve", TMPS[:, :, :], src, bc_last(A[:, k, 1:17], 4), ALU.mult, R=[psk(bank)], W=["TMPS"])
                    tt("dve", DST[:, k, c0:c0 + 64].rearrange("p (b t) -> p b t", t=4), TMPS[:, :, :],
                       bc_last(SH[:, k, 1:17], 4), ALU.add, R=["TMPS"], W=[(dst_key, j)])

        def load_w(dst, src2d, col0, ncols, W, R=(), krows=128):
            K = dst.shape[1]
            for c in range(0, ncols, 512):
                cw = min(512, ncols - c)
                ldc(dst[:, :, c:c + cw], src2d[:, col0 + c:col0 + c + cw].rearrange("(k p) n -> p k n", p=krows),
                    W=W, R=R)

        for l in range(DEPTH):
            xsrc_p = inp("xp") if l == 0 else yp
            xsrc_s = inp("xs") if l == 0 else ys

            def xrows(j):
                c0, n = tcols(j)
                return (xsrc_p[c0:c0 + n, :] if j < 16 else xsrc_s[:, :])

            def yrows(j):
                c0, n = tcols(j)
                return (yp[c0:c0 + n, :] if j < 16 else ys[:, :])

            LD_ = Lay(arena, 28672, OFF_END)
            C17 = LD_.take([1024], F32, parts=32)
            SC17 = LD_.take([1024], BF16, parts=32)
            WB = [LD_.take([8, 512], BF16) for _ in range(2)]
            BADA = LD_.take([6144], F32, parts=17)
            ADA_TM = LD_.take([6144], F32, parts=17)
            GN = LD_.take([1024], F32, parts=4)
            G1B = LD_.take([2, 1024], F32, parts=17)
            if l == 0:
                memset("dve", C17, 0.0, W=["c17"])
                ld(C17[0:17, :], inp("c17"), W=["c17"], R=["c17"])
                act(SC17, C17, AF.Silu, R=["c17"], W=["sc17"])
                pT = ps(0).bitcast(BF16)
                for k in range(8):
                    tr(pT[:, k * 32:k * 32 + 32], SC17[:, k * 128:(k + 1) * 128], IDB[0:32, 0:32], R=["sc17", "idb"],
                       W=[psk(0)], signal=(k == 7))
                cp("dve", SCT, pT[:, 0:256].rearrange("p (k c) -> p k c", c=32), R=[psk(0)], W=["sct"])
            ld(BADA, inp("b_ada")[l:l + 1, :].partition_broadcast(17).rearrange("p a n -> p (a n)"), W=["bada"])
            ld(GN, inp("gain")[l], W=["gn"])
            ld(G1B[:, 0, :], inp("gain")[l, 1:2, :].partition_broadcast(17).rearrange("p a n -> p (a n)"), W=["g1b"])
            ld(G1B[:, 1, :], inp("gain")[l, 3:4, :].partition_broadcast(17).rearrange("p a n -> p (a n)"), W=["g1b"])
            for ng in range(12):
                wb = WB[ng % 2]
                wk = ("wb", ng % 2)
                load_w(wb, inp("w_ada")[l], ng * 512, 512, W=[wk])
                bank = 1 + ng % 2
                for k in range(8):
                    mm(ps(bank)[0:32, :], SCT[:, k, :], wb[:, k, :], k == 0, k == 7, R=["sct", wk], W=[psk(bank)],
                       signal=(k == 7))
                tt("dve", ADA_TM[:, ng * 512:(ng + 1) * 512], ps(bank)[0:17, :], BADA[:, ng * 512:(ng + 1) * 512], ALU.add,
                   R=[psk(bank), "bada"], W=["ada_tm"])
            for half in range(2):
                bank = 3 + half
                for c in range(24):
                    cc = half * 24 + c
                    tr(ps(bank)[:, c * 20:c * 20 + 17], ADA_TM[:, cc * 128:(cc + 1) * 128], IDF[0:17, 0:17],
                       R=["ada_tm", "idf"], W=[psk(bank)], signal=(c == 23))
                cp("dve", ADAT[:, half * 24:(half + 1) * 24, :],
                   ps(bank)[:, 0:480].rearrange("p (c k) -> p c k", k=20)[:, :, 0:17], R=[psk(bank)], W=["adat"])
            for k in range(8):
                tr(ps(5)[:, k * 4:k * 4 + 4], GN[:, k * 128:(k + 1) * 128], IDF[0:4, 0:4], R=["gn", "idf"], W=[psk(5)],
                   signal=(k == 7))
            cp("dve", GNT, ps(5)[:, 0:32].rearrange("p (k c) -> p k c", c=4), R=[psk(5)], W=["gnt"])
            for k in range(8):
                tsc("dve", A_M[:, k, :], ADAT[:, 8 + k, :], 1.0, ALU.add, R=["adat", "gnt"], W=["a_m"],
                    s2=GNT[:, k, 0:1], op1=ALU.mult)
                tsc("dve", A_F[:, k, :], ADAT[:, 32 + k, :], 1.0, ALU.add, R=["adat", "gnt"], W=["a_f"],
                    s2=GNT[:, k, 2:3], op1=ALU.mult)
            SH_M = ADAT[:, 0:8, :]
            SH_F = ADAT[:, 24:32, :]
            tt("dve", ADA_GT[:, 0, :], ADA_TM[:, 2048:3072], G1B[:, 0, :], ALU.mult, R=["ada_tm", "g1b"], W=["ada_gt"])
            tt("dve", ADA_GT[:, 1, :], ADA_TM[:, 5120:6144], G1B[:, 1, :], ALU.mult, R=["ada_tm", "g1b"], W=["ada_gt"])
            P.barrier()
            dump("ada_tm%d" % l, ADA_TM, [17, 6144])
            dump("a_m%d" % l, A_M, [128, 8 * 17])
            if STOP == "ada":
                break

            LD_ = Lay(arena, OFF_DYN, OFF_END)
            LN = {"SQ": LD_.take([1024], BF16), "XN": LD_.take([1024], BF16), "TMPS": LD_.take([16, 4], F32)}
            XT = [LD_.take([1024], F32) for _ in range(2)]
            for j in range(NT):
                c0, n = tcols(j)
                xt = XT[j % 2]
                ld(xt[0:n, :], xrows(j), W=[("xt", j % 2)], R=[("yrow", j)])
                norm_to_T(LN, j, xt, ("xt", j % 2), A_M, SH_M, HT, "ht", bank=j % 2)
            P.barrier()
            dump("ht%d" % l, HT, [128, 8 * TT], BF16)
            if STOP == "norm0":
                break
            w_in_l = inp("w_in")[l]
            LQ = Lay(arena, 114176, OFF_END)
            QT = LQ.take([8, TT], BF16)
            G3 = LQ.take([NT, 48], F32)
            KST = LQ.take([2, TT], BF16)
            KWT = LQ.take([2, TT], BF16)
            VSA = LQ.take([16, 4, 66], BF16)
            VWA = LQ.take([16, 4, 66], BF16)
            KCC = LQ.take([2, 64], BF16)
            VCA = LQ.take([4, 98], BF16, parts=64)
            off_rest = LQ.off
            WBQ = [LQ.take([8, 512], BF16) for _ in range(2)]
            LB = Lay(arena, OFF_YC, OFF_DYN)
            KCT = LB.take([2, TT], BF16)
            VCT = LB.take([2, TT], BF16)
            LO = Lay(arena, OFF_DYN, 114176)
            STG = [LO.take([512], F32) for _ in range(2)]
            W1 = LO.take([2, 32, 128], BF16)
            W2K = LO.take([64], BF16)
            W2V = LO.take([64], BF16)
            B1 = LO.take([2], F32)
            HIDF = LO.take([256], F32)
            HIDG = LO.take([256], F32)
            HID = [LO.take([256], BF16) for _ in range(2)]
            WG48 = LO.take([8, 48], BF16)

            memset("dve", VSA, 0.0, W=["vsa"])
            memset("dve", VWA, 0.0, W=["vwa"])
            memset("dve", VSA[:, :, :, 64:65], 1.0, W=["vsa"])
            memset("dve", VWA[:, :, :, 64:65], 1.0, W=["vwa"])
            TCH = [(0, 512), (512, 512), (1024, 512), (1536, 512), (2048, 64)]
            wbi = [0]

            def next_wb(col0, ncols=512):
                i = wbi[0] % 2
                wbi[0] += 1
                load_w(WBQ[i][:, :, 0:ncols], w_in_l, col0, ncols, W=[("wbq", i)])
                return WBQ[i], ("wbq", i)

            evi = [0]

            def evac(out, in_, R, W, scale=None):
                evi[0] += 1
                if evi[0] % 2 == 0:
                    if scale is None:
                        cp("act", out, in_, R=R, W=W)
                    else:
                        act(out, in_, AF.Identity, R=R, W=W, scale=scale)
                else:
                    if scale is None:
                        cp("dve", out, in_, R=R, W=W)
                    else:
                        tsc("dve", out, in_, scale, ALU.mult, R=R, W=W)

            bki = [0]

            def nbank(lo=0, n=4):
                bki[0] += 1
                return lo + bki[0] % n

            for p_ in range(2):
                wi = wbi[0] % 2
                wbi[0] += 1
                wb, wk = WBQ[wi], ("wbq", wi)
                for i_ in range(4):
                    for h_ in range(2):
                        cs = p_ * 512 + h_ * 256 + i_ * 64
                        ldc(wb[:, :, i_ * 128 + h_ * 64:i_ * 128 + h_ * 64 + 64],
                            w_in_l[:, cs:cs + 64].rearrange("(k p) n -> p k n", p=128), W=[wk])
                for i_ in range(4):
                    ci = 4 * p_ + i_
                    for (t0, n) in TCH:
                        b = nbank()
                        for k in range(8):
                            mm(ps(b)[:, 0:n], wb[:, k, i_ * 128:(i_ + 1) * 128], HT[:, k, t0:t0 + n], k == 0, k == 7, R=[wk, "ht"],
                               W=[psk(b)], signal=(k == 7))
                        evac(QT[:, ci, t0:t0 + n], ps(b)[:, 0:n], R=[psk(b)], W=["qt"], scale=ATTN_SCALE)
            if STOP == "proj_q":
                P.barrier()
                break
            for grp in range(3):
                wb, wk = next_wb(1024 + grp * 512)
                if grp == 0:
                    fm = [(KCT, 0, 0, "kct"), (KCT, 1, 128, "kct"), (VCT, 0, 256, "vct"), (VCT, 1, 384, "vct")]
                elif grp == 1:
                    fm = [(KST, 0, 0, "kst"), (KST, 1, 128, "kst")]
                else:
                    fm = [(KWT, 0, 0, "kwt"), (KWT, 1, 128, "kwt")]
                for (dst, ch, cof, key) in fm:
                    for (t0, n) in TCH:
                        b = nbank()
                        for k in range(8):
                            mm(ps(b)[:, 0:n], wb[:, k, cof:cof + 128], HT[:, k, t0:t0 + n], k == 0, k == 7, R=[wk, "ht"],
                               W=[psk(b)], signal=(k == 7))
                        evac(dst[:, ch, t0:t0 + n], ps(b)[:, 0:n], R=[psk(b)], W=[key])
                for j in range(NT):
                    c0, n = tcols(j)
                    b = nbank()
                    for k in range(8):
                        mm(ps(b)[0:n, :], HT[:, k, c0:c0 + n], wb[:, k, :], k == 0, k == 7, R=[wk, "ht"], W=[psk(b)],
                           signal=(k == 7))
                    sg = STG[j % 2]
                    sgk = ("stg", j % 2)
                    cp("act", sg[0:n, :], ps(b)[0:n, :], R=[psk(b)], W=[sgk])
                    if j < 16:
                        if grp < 2:
                            stv(kvp[l, c0:c0 + n, grp * 512:(grp + 1) * 512], sg[0:n, :], R=[sgk])
                        elif j >= 12:
                            stv(winp[l, c0 - 1536:c0 - 1536 + n, :], sg[0:n, :], R=[sgk])
                        if grp == 1:
                            cp("dve", VSA[:, j, :, 0:64], sg[:, 256:512].rearrange("p (g d) -> p g d", d=64),
                               R=[sgk], W=["vsa"])
                        if grp == 2:
                            cp("dve", VWA[:, j, :, 0:64], sg[:, 256:512].rearrange("p (g d) -> p g d", d=64),
                               R=[sgk], W=["vwa"])
                    else:
                        if grp < 2:
                            stv(kvs[l, :, grp * 512:(grp + 1) * 512], sg[0:64, :], R=[sgk])
                        else:
                            for bb in range(16):
                                stv(wins[l, bb, 508:512, :], sg[4 * bb:4 * bb + 4, :], R=[sgk])
            if STOP == "proj_kv":
                P.barrier()
                break
            ldc(WG48, w_in_l[:, 2560:2608].rearrange("(k p) n -> p k n", p=128), W=["wg48"])
            for j in range(NT):
                c0, n = tcols(j)
                b = nbank()
                for k in range(8):
                    mm(ps(b)[0:n, 0:48], HT[:, k, c0:c0 + n], WG48[:, k, :], k == 0, k == 7, R=["wg48", "ht"], W=[psk(b)],
                       signal=(k == 7))
                act(G3[0:n, j, :], ps(b)[0:n, 0:48], AF.Sigmoid, R=[psk(b)], W=["g3"])
            if STOP == "proj_g":
                P.barrier()
                break
            w1v = inp("w_cmp1")[l]
            for kv in range(2):
                for dup in range(2):
                    ldc(W1[64 * dup:64 * dup + 64, kv, :, :], w1v[kv].rearrange("(pos hd) n -> hd pos n", hd=64),
                        W=["w1"])
            ldc(W2K, inp("w_cmp2")[l, 0], W=["w2k"])
            ldc(W2V, inp("w_cmp2")[l, 1], W=["w2v"])
            P.dma("sp", B1, inp("b_cmp1")[l].rearrange("k n -> n k"), writes=["b1"], allow_slow_non_contiguous=True)

            if STOP == "proj_w":
                P.barrier()
                break

            def compress(SRC_K, SRC_V, kkey, vkey, tok0, KCCd, VCAd, tagk, tagv, nblk=64):
                for kv, SRC, skey in ((0, SRC_K, kkey), (1, SRC_V, vkey)):
                    for par in range(2):
                        b = 4 + par
                        ph = 64 * par
                        first = True
                        for g in (par, par + 2):
                            for pos in range(32):
                                last = (g == par + 2 and pos == 31)
                                mm(ps(b)[:, (g // 2) * 64:(g // 2) * 64 + nblk], W1[ph:ph + 64, kv, pos, :],
                                   SRC[ph:ph + 64, g // 2, tok0 + pos:tok0 + nblk * 32:32], first, last, R=["w1", skey],
                                   W=[psk(b)], signal=last, skip=True)
                                first = False
                    HIDFv = HIDF.rearrange("p (a q b) -> p a q b", a=2, q=2, b=64)
                    for par in range(2):
                        act(HIDFv[:, :, par, :], ps(4 + par)[:, 0:128].rearrange("p (a b) -> p a b", b=64), AF.Identity,
                            R=[psk(4 + par), "b1"], W=["hidf"], bias=B1[:, kv:kv + 1])
                    tt("dve", HIDG, HIDF, HIDF, ALU.mult, R=["hidf"], W=["hidg"])
                    tsc("dve", HIDG, HIDG, 0.044715, ALU.mult, R=["hidg"], W=["hidg"], s2=1.0, op1=ALU.add)
                    tt("dve", HIDG, HIDG, HIDF, ALU.mult, R=["hidg", "hidf"], W=["hidg"])
                    act(HIDG, HIDG, AF.Sigmoid, R=["hidg"], W=["hidg"], scale=1.5957691216057308)
                    tt("dve", HID[kv], HIDG, HIDF, ALU.mult, R=["hidg", "hidf"], W=[("hid", kv)])
                for p_ in range(2):
                    for half in range(2):
                        g = 2 * p_ + half
                        mm(ps(6)[64 * half:64 * half + 64, p_ * 64:p_ * 64 + nblk], W2K, HID[0][:, g * 64:g * 64 + nblk],
                           True, True, R=["w2k", ("hid", 0)], W=[psk(6)], signal=(g == 3))
                cp("dve", KCCd[:, :, 0:nblk], ps(6)[:, 0:128].rearrange("p (a b) -> p a b", b=64)[:, :, 0:nblk],
                   R=[psk(6)], W=[tagk])
                for g in range(4):
                    mm(ps(7)[0:nblk, g * 64:(g + 1) * 64], HID[1][:, g * 64:g * 64 + nblk], W2V, True, True,
                       R=["w2v", ("hid", 1)], W=[psk(7)], signal=(g == 3))
                cp("dve", VCAd[0:nblk, :, 0:64], ps(7)[0:nblk, 0:256].rearrange("p (g d) -> p g d", d=64), R=[psk(7)],
                   W=[tagv])

            for g in range(4):
                cp("dve", VCA[:, g, 64:98], VCAC, R=["vcac"], W=["vca"])
            compress(KCT, VCT, "kct", "vct", 0, KCC, VCA, "kcc", "vca")
            P.barrier()
            dump("qt%d" % l, QT, [128, 8 * TT], BF16)
            dump("kcc%d" % l, KCC, [128, 128], BF16)
            dump("vca%d" % l, VCA, [64, 4 * 98], BF16)
            dump("g3%d" % l, G3, [128, NT * 48])
            if STOP == "proj":
                break

            LA = Lay(arena, off_rest, OFF_END)
            EB = [LA.take([512], BF16) for _ in range(3)]
            SELB = LA.take([4, 32], BF16)
            SELBT = LA.take([4, 128], BF16)
            LA2 = Lay(arena, OFF_YC, OFF_DYN)
            OACC = LA2.take([16, 64], F32)
            TMPO = LA2.take([4, 64], F32)
            PSL = LA2.take([4, 32], F32)
            SCR = LA2.take([4, 32], F32)
            SC2 = LA2.take([32], F32)
            M1 = LA2.take([8], F32)
            M2 = LA2.take([8], F32)
            RD = LA2.take([4], F32)
            WGT = LA2.take([4], F32)
            ONSA = Lay(arena, OFF_DYN, 114176).take([NT, 1024], BF16)
            memset("dve", SELBT, 0.0, W=["selbt"])
            sbi = [0]
            ebi = [0]

            def g3col(G, j, g, br):
                return G[:, j, :].rearrange("p (h b) -> p h b", b=3)[:, 4 * g:4 * g + 4, br]

            def post(accb, ncol, g, br, G3v, nq, first, pslc=False):
                accv = ps(accb)[0:nq, 0:4 * ncol].rearrange("p (h c) -> p h c", c=ncol)
                tsc("dve", RD[0:nq, :], accv[:, :, 64], 1e-30, ALU.max, R=[psk(accb)], W=["rd"])
                recip(RD[0:nq, :], RD[0:nq, :], R=["rd"], W=["rd"])
                if pslc:
                    tsc("dve", PSL[0:nq, g, :], accv[:, 0, 65:97], RD[0:nq, 0:1], ALU.mult, R=[psk(accb), "rd"], W=["psl"])
                    for h in range(1, 4):
                        stt(PSL[0:nq, g, :], accv[:, h, 65:97], RD[0:nq, h:h + 1], PSL[0:nq, g, :], ALU.mult, ALU.add,
                            R=[psk(accb), "rd", "psl"], W=["psl"])
                tt("dve", WGT[0:nq, :], RD[0:nq, :], G3v, ALU.mult, R=["rd", "g3", "g3s"], W=["wgt"])
                if first:
                    tt("dve", OACC[0:nq, 4 * g:4 * g + 4, :], accv[:, :, 0:64], bc_last(WGT[0:nq, :], 64), ALU.mult,
                       R=[psk(accb), "wgt"], W=["oacc"])
                else:
                    tt("dve", TMPO[0:nq, :, :], accv[:, :, 0:64], bc_last(WGT[0:nq, :], 64), ALU.mult,
                       R=[psk(accb), "wgt"], W=["tmpo"])
                    tt("dve", OACC[0:nq, 4 * g:4 * g + 4, :], OACC[0:nq, 4 * g:4 * g + 4, :], TMPO[0:nq, :, :], ALU.add,
                       R=["tmpo", "oacc"], W=["oacc"])

            for j in range(16):
                c0 = j * 128
                nb = min(64, 4 * j + 4)
                for g in range(4):
                    p_, ph = g // 2, 64 * (g % 2)
                    Q = QT[ph:ph + 64, 4 * p_:4 * p_ + 4, c0:c0 + 128]
                    sb = 2 * (g % 2) + sbi[0] % 2
                    sbi[0] += 1
                    mm(ps(sb)[0:nb, :], KCC[ph:ph + 64, p_, 0:nb], Q, True, True, R=["kcc", "qt"], W=[psk(sb)])
                    E = EB[ebi[0] % 3]
                    ek = ("eb", ebi[0] % 3)
                    ebi[0] += 1
                    act(E[0:nb, :], ps(sb)[0:nb, :], AF.Exp, R=[psk(sb)], W=[ek])
                    tt("dve", E[0:nb, :].rearrange("p (h t) -> p h t", t=128), E[0:nb, :].rearrange("p (h t) -> p h t", t=128),
                       bc_mid(CMPMASK[0:nb, c0:c0 + 128], 4), ALU.mult, R=[ek, "cmpmask"], W=[ek])
                    ab = 4
                    for h in range(4):
                        mm(ps(ab)[:, h * 98:(h + 1) * 98], E[0:nb, h * 128:(h + 1) * 128], VCA[0:nb, g, :], h == 0, h == 3,
                           R=[ek, "vca"], W=[psk(ab)], signal=(h == 3), skip=True)
                    post(ab, 98, g, 0, g3col(G3, j, g, 0), 128, True, pslc=(j >= 8))
                if j >= 8:
                    tt("dve", SCR, PSL, bc_mid(SELBIAS[:, j, :], 4), ALU.add, R=["psl", "selbias"], W=["scr"])
                    sb = 7
                    pT = ps(sb).bitcast(BF16)
                    for g in range(4):
                        P.op("dve", lambda e, g=g: e.max(out=M1, in_=SCR[:, g, :]), reads=["scr"], writes=["m1"])
                        P.op("dve", lambda e, g=g: e.match_replace(out=SC2, in_to_replace=M1, in_values=SCR[:, g, :],
                                                                    imm_value=-3.0e9), reads=["scr", "m1"], writes=["sc2"])
                        P.op("dve", lambda e: e.max(out=M2, in_=SC2), reads=["sc2"], writes=["m2"])
                        tsc("dve", SELB[:, g, :], SCR[:, g, :], M2[:, 7:8], ALU.is_lt, R=["scr", "m2"], W=["selb"],
                            s2=-30000.0, op1=ALU.mult)
                        ph = 64 * (g % 2)
                        tr(pT[ph:ph + 32, g * 128:(g + 1) * 128], SELB[:, g, :], IDB, R=["selb", "idb"], W=[psk(sb)])
                        cp("act", SELBT[ph:ph + 32, g, :], pT[ph:ph + 32, g * 128:(g + 1) * 128], R=[psk(sb)], W=["selbt"])
                for br, KT_, VA_, kkey, vkey, kb0 in ((1, KST, VSA, "kst", "vsa", 0), (2, KWT, VWA, "kwt", "vwa", max(0, j - 4))):
                    for g in range(4):
                        p_, ph = g // 2, 64 * (g % 2)
                        Q = QT[ph:ph + 64, 4 * p_:4 * p_ + 4, c0:c0 + 128]
                        ab = 4 + br
                        for kb in range(kb0, j + 1):
                            sb = 2 * (g % 2) + sbi[0] % 2
                            sbi[0] += 1
                            bias = (br == 1 and j >= 8)
                            if bias:
                                mm(ps(sb), MSTX[ph:ph + 64, kb * 128:(kb + 1) * 128], bc_mid(SELBT[ph:ph + 64, g, :], 4),
                                   True, False, R=["mstx", "selbt"], W=[psk(sb)], signal=False)
                            mm(ps(sb), KT_[ph:ph + 64, p_, kb * 128:(kb + 1) * 128], Q, not bias, True, R=[kkey, "qt"],
                               W=[psk(sb)])
                            E = EB[ebi[0] % 3]
                            ek = ("eb", ebi[0] % 3)
                            ebi[0] += 1
                            act(E, ps(sb), AF.Exp, R=[psk(sb)], W=[ek])
                            msk = None
                            if kb == j:
                                msk = TRILE
                            elif br == 2 and kb == j - 4:
                                msk = TRIGT
                            if msk is not None:
                                tt("dve", E.rearrange("p (h t) -> p h t", t=128), E.rearrange("p (h t) -> p h t", t=128),
                                   bc_mid(msk, 4), ALU.mult, R=[ek, "trile", "trigt"], W=[ek])
                            for h in range(4):
                                mm(ps(ab)[:, h * 66:(h + 1) * 66], E[:, h * 128:(h + 1) * 128], VA_[:, kb, g, :],
                                   kb == kb0 and h == 0, kb == j and h == 3, R=[ek, vkey], W=[psk(ab)], signal=(h == 3),
                                   skip=True)
                        post(ab, 66, g, br, g3col(G3, j, g, br), 128, False)
                cp("act", ONSA[:, j, :], OACC.rearrange("p h d -> p (h d)"), R=["oacc"], W=[("onsa", j)])
            memset("dve", ONSA[:, 16, :], 0.0, W=[("onsa", 16)])
            P.barrier()
            dump("onsa%d" % l, ONSA, [128, NT * 1024], BF16)
            if STOP == "attn":
                break
            if not early:
                NROWS = DEPTH * NPOOL_ROWS
                cache2d = inp("cache")
                LS = Lay(arena, 114176, OFF_END)
                QTS = LS.take([8, 64], BF16)
                KSN = LS.take([2, 64], BF16)
                KWN = LS.take([2, 64], BF16)
                WVN = LS.take([8, 512], BF16)
                WG48b = LS.take([8, 48], BF16)
                PG = LS.take([8, 1024], BF16)
                FMb = LS.take([3, 2, 2048], BF16)
                VSAb = LS.take([16, 4, 66], BF16)
                SWb = LS.take([4, 512], BF16)
                KWTb = LS.take([2, 512], BF16)
                VWAb = LS.take([4, 4, 66], BF16)
                KCCb = LS.take([2, 64], BF16)
                VCAb = LS.take([4, 98], BF16, parts=64)
                VNEW = LS.take([2, 4, 66], BF16, parts=4)
                G3b = LS.take([48], F32, parts=4)
                ESB = [LS.take([2, 32], BF16) for _ in range(3)]
                SELB4 = LS.take([4, 32], BF16, parts=4)
                SELBT4 = LS.take([4, 4], BF16)
                SCR4 = LS.take([4, 34], F32, parts=4)
                SC24 = LS.take([34], F32, parts=4)
                M14 = LS.take([8], F32, parts=4)
                M24 = LS.take([8], F32, parts=4)
                RD16 = LS.take([4], F32, parts=16)
                ON16 = LS.take([4, 64], F32, parts=16)
                PS16 = LS.take([4, 32], F32, parts=16)
                OACC4 = LS.take([16, 64], F32, parts=4)
                OB4 = LS.take([1024], BF16, parts=4)
                TMP4 = LS.take([4, 64], F32, parts=4)
                PTF = LS.take([256], F32)
                IOTAF = LS.take([1], F32)
                HIDF = LS.take([256], F32)
                HIDG = LS.take([256], F32)
                HID = [LS.take([256], BF16) for _ in range(2)]
                W2K = LS.take([64], BF16)
                W2V = LS.take([64], BF16)
                B1 = LS.take([2], F32)
                W1 = Lay(arena, OFF_YC, OFF_DYN).take([2, 32, 128], BF16)
                cp("dve", QTS, QT[:, :, 2048:2112], R=["qt"], W=["qts"])
                cp("dve", KSN, KST[:, :, 2048:2112], R=["kst"], W=["ksn"])
                cp("dve", KWN, KWT[:, :, 2048:2112], R=["kwt"], W=["kwn"])
                P.barrier()
                ldc(WVN[:, :, 0:256], w_in_l[:, 1792:2048].rearrange("(k p) n -> p k n", p=128), W=["wvn"])
                ldc(WVN[:, :, 256:512], w_in_l[:, 2304:2560].rearrange("(k p) n -> p k n", p=128), W=["wvn"])
                ldc(WG48b, w_in_l[:, 2560:2608].rearrange("(k p) n -> p k n", p=128), W=["wg48b"])
                for kv in range(2):
                    for dup in range(2):
                        ldc(W1[64 * dup:64 * dup + 64, kv, :, :], w1v[kv].rearrange("(pos hd) n -> hd pos n", hd=64),
                            W=["w1"])
                ldc(W2K, inp("w_cmp2")[l, 0], W=["w2k"])
                ldc(W2V, inp("w_cmp2")[l, 1], W=["w2v"])
                P.dma("sp", B1, inp("b_cmp1")[l].rearrange("k n -> n k"), writes=["b1"], allow_slow_non_contiguous=True)
                ld(PIDX, inp("pt").rearrange("b g -> (b g)").partition_broadcast(128), W=["pidx"])
                cp("dve", PTF, PIDX, R=["pidx"], W=["ptf"])
                cp("dve", IOTAF, IOTA, R=["iota"], W=["iotaf"])
                tsc("dve", PTF, PTF, 128.0, ALU.mult, R=["ptf", "iotaf"], W=["ptf"], s2=IOTAF[:, 0:1], op1=ALU.add)
                tsc("dve", PTF, PTF, float(l * NPOOL_ROWS), ALU.add, R=["ptf"], W=["ptf"])
                cp("dve", PIDX, PTF, R=["ptf"], W=["pidx"])
                memset("dve", VSAb, 0.0, W=["vsab"])
                memset("dve", VSAb[:, :, :, 64:65], 1.0, W=["vsab"])
                memset("dve", VWAb, 0.0, W=["vwab"])
                memset("dve", VWAb[:, :, :, 64:65], 1.0, W=["vwab"])
                memset("dve", VNEW, 0.0, W=["vnew"])
                memset("dve", VNEW[:, :, :, 64:65], 1.0, W=["vnew"])
                memset("dve", SELBT4, 0.0, W=["selbt4"])
                memset("dve", SCR4, 1.0e4, W=["scr4"])
                for g in range(4):
                    cp("dve", VCAb[:, g, 64:98], VCAC, R=["vcac"], W=["vcab"])
                stv(wins[l, :, 0:508, :], inp("swin")[l, :, 4:512, :], R=[])
                esi = [0]
                for bb in range(16):
                    tb = 4 * bb
                    def gather(pg):
                        col = bb * 16 + pg
                        P.dma("pool", PG[:, pg % 8, :], cache2d[:, :], reads=["pidx"], writes=[("pg", pg % 8)],
                              indirect=bass.IndirectOffsetOnAxis(ap=PIDX[:, col:col + 1], axis=0),
                              bounds_check=NROWS - 1, oob_is_err=False)

                    for pg in range(8):
                        gather(pg)
                    ldc(SWb, inp("swin")[l, bb].rearrange("(q p) c -> p q c", p=128), W=["swb"])
                    for pg in range(16):
                        bk = 6 + pg % 2
                        pT = ps(bk).bitcast(BF16)
                        for t3 in range(3):
                            for c in range(2):
                                i6 = t3 * 2 + c
                                tr(pT[:, i6 * 128:(i6 + 1) * 128], PG[:, pg % 8, t3 * 256 + c * 128:t3 * 256 + (c + 1) * 128], IDB,
                                   R=[("pg", pg % 8), "idb"], W=[psk(bk)], signal=(i6 == 5))
                        evac(FMb[:, :, :, pg * 128:(pg + 1) * 128].rearrange("p a c t -> p (a c) t"),
                             pT[:, 0:768].rearrange("p (s t) -> p s t", t=128), R=[psk(bk)], W=["fmb"])
                        cp("dve", VSAb[:, pg, :, 0:64], PG[:, pg % 8, 768:1024].rearrange("p (g d) -> p g d", d=64),
                           R=[("pg", pg % 8)], W=["vsab"])
                        if pg + 8 < 16:
                            gather(pg + 8)
                    for q4 in range(4):
                        bk = 6 + q4 % 2
                        pT = ps(bk).bitcast(BF16)
                        for c in range(2):
                            tr(pT[:, c * 128:(c + 1) * 128], SWb[:, q4, c * 128:(c + 1) * 128], IDB, R=["swb", "idb"],
                               W=[psk(bk)], signal=(c == 1))
                        evac(KWTb[:, :, q4 * 128:(q4 + 1) * 128], pT[:, 0:256].rearrange("p (s t) -> p s t", t=128),
                             R=[psk(bk)], W=["kwtb"])
                        cp("dve", VWAb[:, q4, :, 0:64], SWb[:, q4, 256:512].rearrange("p (g d) -> p g d", d=64), R=["swb"],
                           W=["vwab"])
                    compress(FMb[:, 0], FMb[:, 1], "fmb", "fmb", 0, KCCb, VCAb, "kccb", "vcab")
                    for s2 in range(2):
                        for k in range(8):
                            mm(ps(6)[0:4, s2 * 256:(s2 + 1) * 256], HT[:, k, 2048 + tb:2048 + tb + 4],
                               WVN[:, k, s2 * 256:(s2 + 1) * 256], k == 0, k == 7, R=["ht", "wvn"], W=[psk(6)],
                               signal=(k == 7))
                    cp("dve", VNEW[:, :, :, 0:64], ps(6)[0:4, :].rearrange("p (s g d) -> p s g d", s=2, d=64), R=[psk(6)],
                       W=["vnew"])
                    for k in range(8):
                        mm(ps(7)[0:4, 0:48], HT[:, k, 2048 + tb:2048 + tb + 4], WG48b[:, k, :], k == 0, k == 7,
                           R=["ht", "wg48b"], W=[psk(7)], signal=(k == 7))
                    act(G3b, ps(7)[0:4, 0:48], AF.Sigmoid, R=[psk(7)], W=["g3b"])
                    G3bv = G3b.rearrange("p (g h b) -> p g h b", h=4, b=3)

                    def Qs(g):
                        p_, ph = g // 2, 64 * (g % 2)
                        return QTS[ph:ph + 64, 4 * p_:4 * p_ + 4, tb:tb + 4]

                    def post_s(accb, ncol, br, first, pslc=False):
                        accv = ps(accb)[0:16, 0:4 * ncol].rearrange("p (g c) -> p g c", c=ncol)
                        tsc("dve", RD16, accv[:, :, 64], 1e-30, ALU.max, R=[psk(accb)], W=["rd16"])
                        recip(RD16, RD16, R=["rd16"], W=["rd16"])
                        tt("dve", ON16, accv[:, :, 0:64], bc_last(RD16, 64), ALU.mult, R=[psk(accb), "rd16"], W=["on16"])
                        if pslc:
                            tt("dve", PS16, accv[:, :, 65:97], bc_last(RD16, 32), ALU.mult, R=[psk(accb), "rd16"], W=["ps16"])
                            mm(ps(7)[0:4, 256:384], RTM, PS16.rearrange("p g s -> p (g s)"), True, True, R=["rtm", "ps16"],
                               W=[psk(7)])
                        for h in range(4):
                            mm(ps(7)[0:4, 0:256], RSM[:, h, :], ON16.rearrange("p g d -> p (g d)"), True, True,
                               R=["rsm", "on16"], W=[psk(7)])
                            src = ps(7)[0:4, 0:256].rearrange("p (g d) -> p g d", d=64)
                            dst = OACC4.rearrange("p (g h) d -> p g h d", h=4)[:, :, h, :]
                            gate = bc_last(G3bv[:, :, h, br], 64)
                            if first:
                                tt("dve", dst, src, gate, ALU.mult, R=[psk(7), "g3b"], W=["oacc4"])
                            else:
                                tt("dve", TMP4, src, gate, ALU.mult, R=[psk(7), "g3b"], W=["tmp4"])
                                tt("dve", dst, dst, TMP4, ALU.add, R=["tmp4", "oacc4"], W=["oacc4"])

                    def exp_pair(nk):
                        E = ESB[esi[0] % 3]
                        ek = ("esb", esi[0] % 3)
                        esi[0] += 1
                        for par in range(2):
                            act(E[0:nk, par, :], ps(2 * par)[0:nk, 0:32], AF.Exp, R=[psk(2 * par)], W=[ek])
                        return E, ek

                    for g in range(4):
                        p_, ph = g // 2, 64 * (g % 2)
                        mm(ps(2 * (g % 2))[0:64, (g // 2) * 16:(g // 2) * 16 + 16], KCCb[ph:ph + 64, p_, :], Qs(g), g < 2, g >= 2,
                           R=["kccb", "qts"], W=[psk(2 * (g % 2))], skip=True)
                    E, ek = exp_pair(64)
                    for g in range(4):
                        mm(ps(4)[0:16, g * 98:(g + 1) * 98], E[0:64, g % 2, (g // 2) * 16:(g // 2) * 16 + 16], VCAb[:, g, :],
                           g == 0, g == 3, R=[ek, "vcab"], W=[psk(4)], signal=(g == 3), skip=True)
                    post_s(4, 98, 0, True, pslc=True)
                    tt("dve", SCR4[:, :, 0:32], ps(7)[0:4, 256:384].rearrange("p (g s) -> p g s", s=32),
                       bc_mid(SELBIAS_S[:, 0:32], 4), ALU.add, R=[psk(7), "selbias_s"], W=["scr4"])
                    pT = ps(6).bitcast(BF16)
                    for g in range(4):
                        P.op("dve", lambda e, g=g: e.max(out=M14, in_=SCR4[:, g, 0:33]), reads=["scr4"], writes=["m14"])
                        P.op("dve", lambda e, g=g: e.match_replace(out=SC24[:, 0:33], in_to_replace=M14, in_values=SCR4[:, g, 0:33],
                                                                    imm_value=-3.0e9), reads=["scr4", "m14"], writes=["sc24"])
                        P.op("dve", lambda e: e.max(out=M24, in_=SC24[:, 0:33]), reads=["sc24"], writes=["m24"])
                        tsc("dve", SELB4[:, g, :], SCR4[:, g, 0:32], M24[:, 7:8], ALU.is_lt, R=["scr4", "m24"], W=["selb4"],
                            s2=-30000.0, op1=ALU.mult)
                        ph = 64 * (g % 2)
                        tr(pT[ph:ph + 32, g * 4:g * 4 + 4], SELB4[:, g, :], IDB[0:4, 0:4], R=["selb4", "idb"], W=[psk(6)])
                        cp("act", SELBT4[ph:ph + 32, g, :], pT[ph:ph + 32, g * 4:g * 4 + 4], R=[psk(6)], W=["selbt4"])
                    for br in (1, 2):
                        nkb = 16 if br == 1 else 4
                        for kb in range(nkb + 1):
                            new = (kb == nkb)
                            nk = 4 if new else 128
                            for g in range(4):
                                p_, ph = g // 2, 64 * (g % 2)
                                sbk = 2 * (g % 2)
                                out = ps(sbk)[0:nk, (g // 2) * 16:(g // 2) * 16 + 16]
                                if br == 1:
                                    lhs = KSN[ph:ph + 64, p_, tb:tb + 4] if new else FMb[ph:ph + 64, 2, p_, kb * 128:(kb + 1) * 128]
                                else:
                                    lhs = KWN[ph:ph + 64, p_, tb:tb + 4] if new else KWTb[ph:ph + 64, p_, kb * 128:(kb + 1) * 128]
                                bias = (br == 1 and not new)
                                if bias:
                                    mm(out, MSTX[ph:ph + 64, kb * 128:(kb + 1) * 128], bc_mid(SELBT4[ph:ph + 64, g, :], 4),
                                       g < 2, False, R=["mstx", "selbt4"], W=[psk(sbk)], signal=False, skip=True)
                                mm(out, lhs, Qs(g), (g < 2) and not bias, True, R=["ksn", "kwn", "fmb", "kwtb", "qts"],
                                   W=[psk(sbk)], skip=True)
                            E, ek = exp_pair(nk)
                            Ev = E.rearrange("p a (q h t) -> p a q h t", q=2, h=4)
                            if new:
                                for par in range(2):
                                    for q2 in range(2):
                                        tt("dve", Ev[0:4, par, q2], Ev[0:4, par, q2], bc_mid(TRILE[0:4, 0:4], 4), ALU.mult,
                                           R=[ek, "trile"], W=[ek])
                            elif br == 2 and kb == 0:
                                for par in range(2):
                                    for q2 in range(2):
                                        tt("dve", Ev[:, par, q2], Ev[:, par, q2], bc_mid(WMASK, 4), ALU.mult, R=[ek, "wmask"],
                                           W=[ek])
                            ab = 4 + br
                            for g in range(4):
                                if new:
                                    rhs = VNEW[:, br - 1, g, :]
                                elif br == 1:
                                    rhs = VSAb[:, kb, g, :]
                                else:
                                    rhs = VWAb[:, kb, g, :]
                                mm(ps(ab)[0:16, g * 66:(g + 1) * 66], E[0:nk, g % 2, (g // 2) * 16:(g // 2) * 16 + 16], rhs,
                                   kb == 0 and g == 0, new and g == 3, R=[ek, "vnew", "vsab", "vwab"], W=[psk(ab)],
                                   signal=(g == 3), skip=True)
                        post_s(4 + br, 66, br, False)
                    cp("act", OB4, OACC4.rearrange("p h d -> p (h d)"), R=["oacc4"], W=["ob4"])
                    P.dma("sp", ONSA[tb:tb + 4, 16, :], OB4, reads=["ob4"], writes=[("onsa", 16)])
                P.barrier()
                dump("onsas%d" % l, ONSA[0:64, 16, :], [64, 1024], BF16)
            if STOP == "attns":
                break
            LC = Lay(arena, 114176, OFF_END)
            WGL = LC.take([8, 1024], BF16)
            CIN = LC.take([4, 2078], BF16)
            CINS = LC.take([4, 16, 34], BF16)
            ACC = LC.take([4, 1024], F32)
            ACCS = LC.take([4, 64], F32)
            CVT = LC.take([4, 34], F32)
            CV34 = LC.take([512], F32, parts=34)
            MEAN = LC.take([512], F32)
            RSTD = LC.take([512], F32)
            SQT = [LC.take([512], F32) for _ in range(2)]
            ZT = LC.take([512], F32)
            SIGB = [LC.take([512], F32) for _ in range(2)]
            GLS = LC.take([512], F32)
            SCS = LC.take([4, 512], F32, parts=120)
            load_w(WGL, w_in_l, 2608, 1024, W=["wgl"])
            ld(CV34, inp("cvp")[l], W=["cv34"])
            for c in range(4):
                tr(ps(0)[:, c * 34:(c + 1) * 34], CV34[:, c * 128:(c + 1) * 128], IDF[0:34, 0:34], R=["cv34", "idf"],
                   W=[psk(0)], signal=(c == 3))
            cp("dve", CVT, ps(0)[:, 0:136].rearrange("p (c k) -> p c k", k=34), R=[psk(0)], W=["cvt"])
            memset("dve", CIN[:, :, 0:30], 0.0, W=["cin"])
            for c in range(4):
                for (t0, n) in TCH:
                    ba = nbank(0, 2)
                    bb_ = 2 + ba
                    for k in range(8):
                        mm(ps(ba)[:, 0:n], WGL[:, k, c * 128:(c + 1) * 128], HT[:, k, t0:t0 + n], k == 0, k == 7,
                           R=["wgl", "ht"], W=[psk(ba)], signal=(k == 7))
                    for k in range(8):
                        mm(ps(bb_)[:, 0:n], WGL[:, k, 512 + c * 128:512 + (c + 1) * 128], HT[:, k, t0:t0 + n], k == 0, k == 7,
                           R=["wgl", "ht"], W=[psk(bb_)], signal=(k == 7))
                    sg = SIGB[ba]
                    act(sg[:, 0:n], ps(bb_)[:, 0:n], AF.Sigmoid, R=[psk(bb_)], W=[("sigb", ba)])
                    if t0 < 2048:
                        tt("dve", CIN[:, c, 30 + t0:30 + t0 + n], ps(ba)[:, 0:n], sg[:, 0:n], ALU.mult,
                           R=[psk(ba), ("sigb", ba)], W=["cin"])
                    else:
                        tt("dve", CINS[:, c, :, 30:34], ps(ba)[:, 0:64].rearrange("p (b t) -> p b t", t=4),
                           sg[:, 0:64].rearrange("p (b t) -> p b t", t=4), ALU.mult, R=[psk(ba), ("sigb", ba)], W=["cins"])
            for j in (15, 16):
                c0, n = tcols(j)
                for k in range(8):
                    mm(ps(4)[0:n, :], HT[:, k, c0:c0 + n], WGL[:, k, 0:512], k == 0, k == 7, R=["wgl", "ht"], W=[psk(4)],
                       signal=(k == 7))
                for k in range(8):
                    mm(ps(5)[0:n, :], HT[:, k, c0:c0 + n], WGL[:, k, 512:1024], k == 0, k == 7, R=["wgl", "ht"], W=[psk(5)],
                       signal=(k == 7))
                act(SIGB[0][0:n, :], ps(5)[0:n, :], AF.Sigmoid, R=[psk(5)], W=[("sigb", 0)])
                tt("dve", GLS[0:n, :], ps(4)[0:n, :], SIGB[0][0:n, :], ALU.mult, R=[psk(4), ("sigb", 0)], W=["gls"])
                if j == 15:
                    stv(convp[l], GLS[98:128, :], R=["gls"])
                else:
                    for bb in range(16):
                        stv(convs[l, bb, 26:30, :], GLS[4 * bb:4 * bb + 4, :], R=["gls"])
            if not early:
                scv = inp("sconv")[l].rearrange("(b r) c -> b r c", r=30)
                stv(convs[l, :, 0:26, :], scv[:, 4:30, :], R=[])
                ld(SCS, inp("sconv")[l].rearrange("(q r) c -> r q c", r=120), W=["scs"])
                for q in range(4):
                    for c in range(4):
                        b = nbank(0, 4)
                        tr(ps(b)[:, 0:120], SCS[:, q, c * 128:(c + 1) * 128], IDF[0:120, 0:120], R=["scs", "idf"],
                           W=[psk(b)])
                        cp("dve", CINS[:, c, 4 * q:4 * q + 4, 0:30], ps(b)[:, 0:120].rearrange("p (b r) -> p b r", r=30),
                           R=[psk(b)], W=["cins"])
            else:
                memset("dve", CINS[:, :, :, 0:30], 0.0, W=["cins"])

            def layer_norm_chunk(ACCv, n, t0):
                for c in range(4):
                    mm(ps(4)[:, 0:n], ONESF, ACCv[:, c, :], c == 0, c == 3, R=["onesf", "acc"], W=[psk(4)], signal=(c == 3))
                for c in range(4):
                    act(SQT[c % 2][:, 0:n], ACCv[:, c, :], AF.Square, R=["acc"], W=[("sqt", c % 2)])
                    mm(ps(5)[:, 0:n], ONESF, SQT[c % 2][:, 0:n], c == 0, c == 3, R=["onesf", ("sqt", c % 2)], W=[psk(5)])
                act(MEAN[:, 0:n], ps(4)[:, 0:n], AF.Identity, R=[psk(4)], W=["mean"], scale=1.0 / 512)
                tt("dve", ZT[:, 0:n], MEAN[:, 0:n], MEAN[:, 0:n], ALU.mult, R=["mean"], W=["zt"])
                stt(RSTD[:, 0:n], ps(5)[:, 0:n], 1.0 / 512, ZT[:, 0:n], ALU.mult, ALU.subtract, R=[psk(5), "zt"], W=["rstd"])
                act(RSTD[:, 0:n], RSTD[:, 0:n], AF.Sqrt, R=["rstd"], W=["rstd"], bias=EPS)
                recip(RSTD[:, 0:n], RSTD[:, 0:n], R=["rstd"], W=["rstd"])
                for c in range(4):
                    tt("dve", ZT[:, 0:n], ACCv[:, c, :], MEAN[:, 0:n], ALU.subtract, R=["acc", "mean"], W=["zt"])
                    tt("dve", ZT[:, 0:n], ZT[:, 0:n], RSTD[:, 0:n], ALU.mult, R=["zt", "rstd"], W=["zt"])
                    tsc("dve", ZT[:, 0:n], ZT[:, 0:n], CVT[:, c, 32:33], ALU.mult, R=["zt", "cvt"], W=["zt"],
                        s2=CVT[:, c, 33:34], op1=ALU.add)
                    act(YC[:, c, t0:t0 + n], ZT[:, 0:n], AF.Silu, R=["zt"], W=["yc"])

            for half in range(2):
                h0 = half * 1024
                for c in range(4):
                    tsc("dve", ACC[:, c, :], CIN[:, c, h0:h0 + 1024], CVT[:, c, 0:1], ALU.mult, R=["cin", "cvt"], W=["acc"],
                        s2=CVT[:, c, 31:32], op1=ALU.add)
                    for k in range(1, 31):
                        stt(ACC[:, c, :], CIN[:, c, h0 + k:h0 + k + 1024], CVT[:, c, k:k + 1], ACC[:, c, :], ALU.mult, ALU.add,
                            R=["cin", "cvt", "acc"], W=["acc"])
                for q in range(2):
                    layer_norm_chunk(ACC[:, :, q * 512:(q + 1) * 512], 512, h0 + q * 512)
            ACCSv = ACCS.rearrange("p c (b t) -> p c b t", t=4)
            for c in range(4):
                tsc("dve", ACCSv[:, c, :, :], CINS[:, c, :, 0:4], CVT[:, c, 0:1], ALU.mult, R=["cins", "cvt"], W=["acc"],
                    s2=CVT[:, c, 31:32], op1=ALU.add)
                for k in range(1, 31):
                    stt(ACCSv[:, c, :, :], CINS[:, c, :, k:k + 4], CVT[:, c, k:k + 1], ACCSv[:, c, :, :], ALU.mult, ALU.add,
                        R=["cins", "cvt", "acc"], W=["acc"])
            layer_norm_chunk(ACCS, 64, 2048)
            P.barrier()
            dump("yc%d" % l, YC, [128, 4 * TT], BF16)
            if STOP == "conv":
                break

            LM = Lay(arena, 114176, OFF_END)
            WMG = LM.take([8, 2048], BF16)
            WPW = LM.take([4, 1024], BF16)
            WOUT = LM.take([8, 1024], BF16)
            GATE = LM.take([1024], F32)
            T1 = LM.take([1024], F32)
            MB = LM.take([1024], BF16)
            MT = LM.take([8, 128], BF16)
            XT = [LM.take([1024], F32) for _ in range(2)]
            YN = LM.take([1024], F32)
            SQJ = LM.take([1024], BF16)
            load_w(WMG, w_in_l, 3632, 2048, W=["wmg"])
            load_w(WPW, inp("w_pw2")[l], 0, 1024, W=["wpw"])
            load_w(WOUT, inp("w_out")[l], 0, 1024, W=["wout"])

            def residual(j, n, yb0, yb1, gb0, gb1, gi, xt, xk, SQJ_, YN_):
                sel = SELP if j < 16 else SELS
                act(SQJ_[0:n, 0:512], ps(yb0)[0:n, :], AF.Square, R=[psk(yb0)], W=["sqj", "ssa"], accum=SS[0:n, 2:3])
                act(SQJ_[0:n, 512:1024], ps(yb1)[0:n, :], AF.Square, R=[psk(yb1)], W=["sqj", "ssb"], accum=SS[0:n, 3:4])
                tt("dve", SS[0:n, 4:5], SS[0:n, 2:3], SS[0:n, 3:4], ALU.add, R=["ssa", "ssb"], W=["ssc"])
                act(SS[0:n, 4:5], SS[0:n, 4:5], AF.Sqrt, R=["ssc"], W=["ssc"], scale=1.0 / D, bias=EPS)
                recip(SS[0:n, 4:5], SS[0:n, 4:5], R=["ssc"], W=["ssc"])
                for hh, (yb, gb) in enumerate(((yb0, gb0), (yb1, gb1))):
                    mm(ps(gb)[0:n, :], sel[:, 0:n], ADA_GT[:, gi, hh * 512:(hh + 1) * 512], True, True,
                       R=["selp", "sels", "ada_gt"], W=[psk(gb)])
                    act(YN_[0:n, hh * 512:(hh + 1) * 512], ps(yb)[0:n, :], AF.Identity, R=[psk(yb), "ssc"], W=["yn"],
                        scale=SS[0:n, 4:5])
                    tt("dve", YN_[0:n, hh * 512:(hh + 1) * 512], YN_[0:n, hh * 512:(hh + 1) * 512], ps(gb)[0:n, :], ALU.mult,
                       R=["yn", psk(gb)], W=["yn"])
                tt("dve", xt[0:n, :], xt[0:n, :], YN_[0:n, :], ALU.add, R=[xk, "yn"], W=[xk])
                stv(yrows(j), xt[0:n, :], R=[xk], W=[("yrow", j)])

            for j in range(NT):
                c0, n = tcols(j)
                xt = XT[j % 2]
                xk = ("xt", j % 2)
                ld(xt[0:n, :], xrows(j), W=[xk])
                for hh in range(2):
                    for k in range(8):
                        mm(ps(hh)[0:n, :], HT[:, k, c0:c0 + n], WMG[:, k, hh * 512:(hh + 1) * 512], k == 0, k == 7,
                           R=["ht", "wmg"], W=[psk(hh)], signal=(k == 7))
                    act(GATE[0:n, hh * 512:(hh + 1) * 512], ps(hh)[0:n, :], AF.Sigmoid, R=[psk(hh)], W=["gate"])
                tt("dve", T1[0:n, :], GATE[0:n, :], ONSA[0:n, j, :], ALU.mult, R=["gate", ("onsa", j)], W=["t1"])
                for hh in range(2):
                    for k in range(8):
                        mm(ps(hh)[0:n, :], HT[:, k, c0:c0 + n], WMG[:, k, 1024 + hh * 512:1024 + (hh + 1) * 512], k == 0,
                           k == 7, R=["ht", "wmg"], W=[psk(hh)], signal=(k == 7))
                    act(GATE[0:n, hh * 512:(hh + 1) * 512], ps(hh)[0:n, :], AF.Sigmoid, R=[psk(hh)], W=["gate"])
                    for c in range(4):
                        mm(ps(2 + hh)[0:n, :], YC[:, c, c0:c0 + n], WPW[:, c, hh * 512:(hh + 1) * 512], c == 0, c == 3,
                           R=["yc", "wpw"], W=[psk(2 + hh)], signal=(c == 3))
                    tt("dve", GATE[0:n, hh * 512:(hh + 1) * 512], GATE[0:n, hh * 512:(hh + 1) * 512], ps(2 + hh)[0:n, :],
                       ALU.mult, R=["gate", psk(2 + hh)], W=["gate"])
                tt("dve", MB[0:n, :], T1[0:n, :], GATE[0:n, :], ALU.add, R=["t1", "gate"], W=["mb"])
                pT = ps(4).bitcast(BF16)
                for k in range(8):
                    tr(pT[:, k * 128:k * 128 + n], MB[0:n, k * 128:(k + 1) * 128], IDB[0:n, 0:n], R=["mb", "idb"], W=[psk(4)],
                       signal=(k == 7))
                cp("act", MT[:, :, 0:n], pT.rearrange("p (k t) -> p k t", t=128)[:, :, 0:n], R=[psk(4)], W=["mt"])
                for hh in range(2):
                    for k in range(8):
                        mm(ps(5 + hh)[0:n, :], MT[:, k, 0:n], WOUT[:, k, hh * 512:(hh + 1) * 512], k == 0, k == 7,
                           R=["mt", "wout"], W=[psk(5 + hh)], signal=(k == 7))
                residual(j, n, 5, 6, 7, 3, 0, xt, xk, SQJ, YN)
            P.barrier()
            if STOP == "merge":
                break

            LF = Lay(arena, 28672, OFF_END)
            WO = LF.take([NFC, 1024], BF16)
            H2T = LF.take([8, 512], BF16)
            UT = LF.take([NFC, 512], BF16)
            WA = [LF.take([8, 256], BF16) for _ in range(2)]
            WB_ = [LF.take([8, 256], BF16) for _ in range(2)]
            XC = LF.take([4, 1024], F32)
            LNF = {"SQ": LF.take([1024], BF16), "XN": LF.take([1024], BF16), "TMPS": LF.take([16, 4], F32)}
            SIL = [LF.take([512], F32) for _ in range(2)]
            YNF = LF.take([1024], F32)
            SQF = LF.take([1024], BF16)
            w_f1_l = inp("w_f1")[l]
            for c in range(0, 1024, 512):
                ldc(WO[:, :, c:c + 512], inp("w_f2")[l][:, c:c + 512].rearrange("(k p) n -> p k n", p=128), W=["wo"])
            for tc in range(5):
                tiles = [4 * tc + i for i in range(4)] if tc < 4 else [16]
                ntok = 512 if tc < 4 else 64
                for i, j in enumerate(tiles):
                    c0, n = tcols(j)
                    ld(XC[0:n, i, :], yrows(j), W=[("xc", i)], R=[("yrow", j)])
                    norm_to_T(LNF, j, XC[:, i, :], ("xc", i), A_F, SH_F, H2T, "h2t", bank=6 + i % 2, dcol=i * 128)
                for fg in range(11):
                    wa = WA[fg % 2]
                    wb2 = WB_[fg % 2]
                    load_w(wa, w_f1_l, fg * 256, 256, W=[("wa", fg % 2)])
                    load_w(wb2, w_f1_l, D_FFN + fg * 256, 256, W=[("wb_", fg % 2)])
                    for f2 in range(2):
                        fc = 2 * fg + f2
                        ba = f2
                        bb_ = 2 + f2
                        for k in range(8):
                            mm(ps(ba)[:, 0:ntok], wa[:, k, f2 * 128:(f2 + 1) * 128], H2T[:, k, 0:ntok], k == 0, k == 7,
                               R=[("wa", fg % 2)] + [("h2t", jj) for jj in tiles], W=[psk(ba)], signal=(k == 7))
                        for k in range(8):
                            mm(ps(bb_)[:, 0:ntok], wb2[:, k, f2 * 128:(f2 + 1) * 128], H2T[:, k, 0:ntok], k == 0, k == 7,
                               R=[("wb_", fg % 2)] + [("h2t", jj) for jj in tiles], W=[psk(bb_)], signal=(k == 7))
                        act(SIL[f2][:, 0:ntok], ps(ba)[:, 0:ntok], AF.Silu, R=[psk(ba)], W=[("sil", f2)])
                        tt("dve", UT[:, fc, 0:ntok], SIL[f2][:, 0:ntok], ps(bb_)[:, 0:ntok], ALU.mult,
                           R=[("sil", f2), psk(bb_)], W=["ut"])
                for i, j in enumerate(tiles):
                    c0, n = tcols(j)
                    for hh in range(2):
                        for fc in range(NFC):
                            mm(ps(4 + hh)[0:n, :], UT[:, fc, i * 128:i * 128 + n], WO[:, fc, hh * 512:(hh + 1) * 512], fc == 0,
                               fc == NFC - 1, R=["ut", "wo"], W=[psk(4 + hh)], signal=(fc == NFC - 1))
                    residual(j, n, 4, 5, 6, 7, 1, XC[:, i, :], ("xc", i), SQF, YNF)
            P.barrier()

        P.barrier()
        P.emit(st)
    return nc, dbg_out


def make_consts():
    bf = ml_dtypes.bfloat16
    c = {}
    c["k_identb"] = np.eye(128, dtype=np.float32).astype(bf)
    c["k_identf"] = np.eye(128, dtype=np.float32)
    p = np.arange(128)[:, None]
    f = np.arange(128)[None, :]
    c["k_trile"] = (p <= f).astype(np.float32).astype(bf)
    c["k_trigt"] = (p > f).astype(np.float32).astype(bf)
    m = np.zeros((128, 2048), np.float32)
    r = np.arange(32)[:, None]
    cc = np.arange(2048)[None, :]
    mst2 = (cc // 64 == r).astype(np.float32)
    m[0:32] = mst2
    m[64:96] = mst2
    c["k_mstx"] = m.astype(bf)
    i = np.arange(64)[:, None]
    c["k_cmpmask"] = (cc >= 32 * i + 31).astype(np.float32).astype(bf)
    v = np.zeros((64, 34), np.float32)
    v[:, 0] = 1.0
    v[:, 1:33] = (np.arange(64)[:, None] // 2 == np.arange(32)[None, :])
    c["k_vcac"] = v.astype(bf)
    sb = np.zeros((128, 16, 32), np.float32)
    for j in range(16):
        for pp in range(128):
            cur = (128 * j + pp) // 64
            s = np.arange(32)
            forced = (s == 0) | (s == cur) | (s == cur - 1)
            valid = s <= cur
            sb[pp, j] = np.where(valid, np.where(forced, 1e4, 0.0), -1e9)
    c["k_selbias"] = sb.reshape(128, 512)
    ss = np.zeros((4, 33), np.float32)
    ss[:, [0, 31, 32]] = 1e4
    c["k_selbias_s"] = ss
    sp = np.zeros((17, 128), np.float32)
    sp[0] = 1.0
    c["k_selp"] = sp
    s_ = np.zeros((17, 64), np.float32)
    for b in range(16):
        s_[1 + b, 4 * b:4 * b + 4] = 1.0
    c["k_sels"] = s_
    rs = np.zeros((16, 4, 4), np.float32)
    rt = np.zeros((16, 4), np.float32)
    for h in range(4):
        for t in range(4):
            rs[h * 4 + t, h, t] = 1.0
            rt[h * 4 + t, t] = 1.0
    c["k_rs"] = rs.reshape(16, 16)
    c["k_rt"] = rt
    c["k_wmask"] = (np.arange(128)[:, None] >= np.arange(4)[None, :] + 1).astype(np.float32).astype(bf)
    c["k_onesf"] = np.ones((128, 128), np.float32)
    c["k_iota"] = np.arange(128, dtype=np.int32).reshape(128, 1)
    return c


def make_in_maps(inp):
    consts = make_consts()
    f32 = lambda a: np.ascontiguousarray(a, dtype=np.float32)
    cache = inp["cache_kv"]
    if cache.size == DEPTH * NPOOL_ROWS * 1024:
        cache = f32(cache).reshape(DEPTH * NPOOL_ROWS, 1024)
    cvp = np.ascontiguousarray(np.concatenate([inp["w_dw"], inp["b_dw"][:, None, :], inp["ln_conv_g"][:, None, :],
                                               inp["ln_conv_b"][:, None, :]], axis=1), dtype=np.float32)
    shared = {
        "cache": cache, "w_ada": f32(inp["w_ada"]), "b_ada": f32(inp["b_ada"]), "gain": f32(inp["norm_gain"]),
        "w_in": f32(inp["w_in"]), "w_cmp1": f32(inp["w_cmp1"]), "b_cmp1": f32(inp["b_cmp1"]), "w_cmp2": f32(inp["w_cmp2"]),
        "cvp": cvp, "w_pw2": f32(inp["w_pw2"]), "w_out": f32(inp["w_out"]), "w_f1": f32(inp["w_ffn_in"]),
        "w_f2": f32(inp["w_ffn_out"]),
    }
    shared.update(consts)
    maps = []
    for i in range(8):
        b0 = 16 * i
        m = dict(shared)
        m["xp"] = f32(inp["x_prompt"][i])
        m["xs"] = f32(inp["x_sample"][b0:b0 + 16]).reshape(64, D)
        m["c17"] = np.ascontiguousarray(np.concatenate([inp["c_prompt"][i:i + 1], inp["c_sample"][b0:b0 + 16]], 0), np.float32)
        m["pt"] = np.ascontiguousarray(inp["page_table"][b0:b0 + 16], dtype=np.int32)
        m["swin"] = f32(inp["state_win"][:, b0:b0 + 16]).reshape(DEPTH, 16, 512, 512)
        m["sconv"] = f32(inp["state_conv"][:, b0:b0 + 16]).reshape(DEPTH, 480, 512)
        maps.append(m)
    return maps


_NC_CACHE = {}


def kernel(**inputs):
    if "nc" not in _NC_CACHE:
        _NC_CACHE["nc"] = build_program()
    nc, _ = _NC_CACHE["nc"]
    maps = make_in_maps(inputs)
    used = set(a.memorylocations[0].name for a in nc.allocations
               if isinstance(a, mybir.MemoryLocationSet) and a.kind == "ExternalInput")
    maps = [{k: v for k, v in m.items() if k in used} for m in maps]
    res = run_bass_kernel_spmd(nc, maps, core_ids=list(range(8))).results
    y_p = np.stack([r["yp"] for r in res], 0)
    y_s = np.concatenate([r["ys"].reshape(16, 4, D) for r in res], 0)
    kv_p = np.stack([r["kvp"].reshape(DEPTH, 2048, 4, 4, 64) for r in res], 1)
    kv_s = np.concatenate([r["kvs"].reshape(DEPTH, 16, 4, 4, 4, 64) for r in res], 1)
    win_p = np.stack([r["winp"].reshape(DEPTH, 512, 2, 4, 64) for r in res], 1)
    win_s = np.concatenate([r["wins"].reshape(DEPTH, 16, 512, 2, 4, 64) for r in res], 1)
    conv_p = np.stack([r["convp"] for r in res], 1)
    conv_s = np.concatenate([r["convs"] for r in res], 1)
    return (y_p, y_s, kv_p, kv_s, win_p, win_s, conv_p, conv_s)
```

```python
import numpy as np
import ml_dtypes
import concourse.bass as bass
import concourse.mybir as mybir
from concourse.bass_utils import run_bass_kernel_spmd
from contextlib import ExitStack

F32 = mybir.dt.float32
BF16 = mybir.dt.bfloat16
I32 = mybir.dt.int32
AF = mybir.ActivationFunctionType
ALU = mybir.AluOpType

SAME_ENG_SYNC = True

D = 1024
KD = 8
NT = 17
TT = 2112
DEPTH = 2
N_IN = 5680
EPS = 1e-6
ATTN_SCALE = 0.125
NPOOL_ROWS = 2560 * 128
D_FFN = 2816
NFC = 22
STOP = None
NO_SAMPLE = False
DBG = {}


def tcols(j):
    return (j * 128, 128) if j < 16 else (2048, 64)


class Prog:
    ENGS = ("pe", "act", "dve", "pool", "sp")

    def __init__(self, nc):
        self.nc = nc
        self.ops = {e: [] for e in self.ENGS}
        self.cnt = {e: 0 for e in self.ENGS}
        self.waited = {e: {} for e in self.ENGS}
        self.state = {}
        self.dma_cnt = {}
        self.dma_seq = {}
        self.regcache = {}
        self.sems = {}
        self.n_instr = 0

    def _deps(self, reads, writes):
        deps = {}

        def add(sv):
            s, v = sv
            if deps.get(s, 0) < v:
                deps[s] = v

        for k in reads:
            st = self.state.get(k)
            if st is not None and st[0] is not None:
                add(st[0])
        for k in writes:
            st = self.state.get(k)
            if st is not None:
                if st[0] is not None:
                    add(st[0])
                for sv in st[1].items():
                    add(sv)
        return deps

    def _filter(self, eng, deps):
        own = "E:" + eng
        own_cnt = self.cnt[eng]
        w = self.waited[eng]
        out = []
        for s, v in deps.items():
            if s == own:
                if v > own_cnt:
                    continue
                if not SAME_ENG_SYNC or eng == "pe":
                    continue
            if w.get(s, 0) >= v:
                continue
            w[s] = v
            out.append((s, v))
        return out

    def _update(self, reads, writes, sv):
        s, v = sv
        for k in reads:
            st = self.state.setdefault(k, [None, {}])
            if st[1].get(s, 0) < v:
                st[1][s] = v
        for k in writes:
            self.state[k] = [sv, {}]

    def op(self, eng, fn, reads=(), writes=(), signal=True):
        deps = self._deps(reads, writes)
        waits = self._filter(eng, deps)
        sk = "E:" + eng
        idx = self.cnt[eng] + 1
        if signal:
            self.cnt[eng] = idx
        self.ops[eng].append((waits, fn, sk if signal else None, 1))
        self._update(reads, writes, (sk, idx))
        self.n_instr += 1

    NRING = 24

    def dma(self, q, out, in_, reads=(), writes=(), stream="d", indirect=None, **kw):
        i = self.dma_seq.get(q, 0)
        self.dma_seq[q] = i + 1
        sk = "D:%s:%02d" % (q, i % self.NRING)
        prev = self.dma_cnt.get(sk, 0)
        val = prev + 16
        self.dma_cnt[sk] = val
        deps = self._deps(reads, writes)
        if prev > 0:
            if deps.get(sk, 0) < prev:
                deps[sk] = prev
        waits = self._filter(q, deps)
        if indirect is None:
            fn = lambda e: e.dma_start(out=out, in_=in_, **kw)
        else:
            regc = self.regcache

            def fn(e):
                kw2 = dict(kw)
                bc = kw2.get("bounds_check")
                if isinstance(bc, int):
                    if bc not in regc:
                        regc[bc] = e.to_reg(bc)
                    kw2["bounds_check"] = regc[bc]
                return e.indirect_dma_start(out=out, out_offset=None, in_=in_, in_offset=indirect, **kw2)
        self.ops[q].append((waits, fn, sk, 16))
        self._update(reads, writes, (sk, val))
        self.n_instr += 1

    def barrier(self):
        allv = {("E:" + e): c for e, c in self.cnt.items() if c > 0}
        allv.update(self.dma_cnt)
        for e in self.ENGS:
            w = self.waited[e]
            waits = []
            for s, v in allv.items():
                if w.get(s, 0) < v:
                    w[s] = v
                    waits.append((s, v))
            if waits:
                self.ops[e].append((waits, None, None, 0))
        self.state = {}

    def emit(self, stack):
        nc = self.nc
        keys = ["E:" + e for e in self.ENGS] + sorted(self.dma_cnt.keys())
        for i, k in enumerate(keys):
            self.sems[k] = stack.enter_context(nc.semaphore("sm%d" % i))
        block = stack.enter_context(nc.Block())
        regs = {"pe": block.tensor, "act": block.scalar, "dve": block.vector,
                "pool": block.gpsimd, "sp": block.sync}
        sems = self.sems
        for e in self.ENGS:
            ops = self.ops[e]

            def body(engine, ops=ops):
                for waits, fn, sk, inc in ops:
                    for (s, v) in waits:
                        engine.wait_ge(sems[s], v)
                    if fn is not None:
                        ins = fn(engine)
                        if sk is not None:
                            ins.then_inc(sems[sk], inc)

            regs[e](body)


class Lay:
    def __init__(self, arena, start, end):
        self.arena = arena
        self.off = start
        self.end = end

    def take(self, free_shape, dt, parts=128, p0=0):
        n = 1
        for s in free_shape:
            n *= s
        esz = 4 if dt in (F32, I32) else 2
        nbytes = (n * esz + 31) // 32 * 32
        o = self.off
        assert o % 4 == 0
        self.off += nbytes
        assert self.off <= self.end, ("arena overflow", self.off, self.end)
        v = self.arena[p0:p0 + parts, o // 2:(o + n * esz) // 2]
        if esz == 4:
            v = v.bitcast(dt)
        if len(free_shape) == 2:
            v = v.rearrange("p (a b) -> p a b", b=free_shape[1])
        elif len(free_shape) == 3:
            v = v.rearrange("p (a b c) -> p a b c", b=free_shape[1], c=free_shape[2])
        elif len(free_shape) == 4:
            v = v.rearrange("p (a b c d) -> p a b c d", b=free_shape[1], c=free_shape[2], d=free_shape[3])
        return v


def bc_last(ap, n):
    shp = list(ap.shape)
    return ap.unsqueeze(len(shp)).broadcast_to(shp + [n])


def bc_mid(ap, n):
    shp = list(ap.shape)
    return ap.unsqueeze(1).broadcast_to([shp[0], n] + shp[1:])


def build_program(dbg_names=()):
    nc = bass.Bass("TRN2", target_bir_lowering=False)
    P = Prog(nc)
    early = NO_SAMPLE

    def din(name, shape, dt=F32):
        return nc.dram_tensor(name, list(shape), dt, kind="ExternalInput")

    def dout(name, shape, dt=F32):
        return nc.dram_tensor(name, list(shape), dt, kind="ExternalOutput")

    SPECS = {
        "xp": ([2048, D], F32), "xs": ([64, D], F32), "c17": ([17, D], F32),
        "cache": ([DEPTH * NPOOL_ROWS, 1024], F32), "pt": ([16, 16], I32),
        "swin": ([DEPTH, 16, 512, 512], F32), "sconv": ([DEPTH, 480, 512], F32),
        "w_ada": ([DEPTH, D, 6144], F32), "b_ada": ([DEPTH, 6144], F32), "gain": ([DEPTH, 4, D], F32),
        "w_in": ([DEPTH, D, N_IN], F32), "w_cmp1": ([DEPTH, 2, 2048, 128], F32), "b_cmp1": ([DEPTH, 2, 128], F32),
        "w_cmp2": ([DEPTH, 2, 128, 64], F32), "cvp": ([DEPTH, 34, 512], F32), "w_pw2": ([DEPTH, 512, D], F32),
        "w_out": ([DEPTH, D, D], F32), "w_f1": ([DEPTH, D, 2 * D_FFN], F32), "w_f2": ([DEPTH, D_FFN, D], F32),
        "k_identb": ([128, 128], BF16), "k_identf": ([128, 128], F32), "k_trile": ([128, 128], BF16),
        "k_trigt": ([128, 128], BF16), "k_mstx": ([128, 2048], BF16), "k_cmpmask": ([64, 2048], BF16),
        "k_vcac": ([64, 34], BF16), "k_selbias": ([128, 512], F32), "k_selbias_s": ([4, 33], F32),
        "k_selp": ([17, 128], F32), "k_sels": ([17, 64], F32), "k_rs": ([16, 16], F32), "k_rt": ([16, 4], F32),
        "k_wmask": ([128, 4], BF16), "k_onesf": ([128, 128], F32), "k_iota": ([128, 1], I32),
    }
    _INS = {}

    def inp(name):
        if name not in _INS:
            shape, dt = SPECS[name]
            _INS[name] = din(name, shape, dt)
        return _INS[name].ap()

    yp = dout("yp", [2048, D]).ap()
    ys = dout("ys", [64, D]).ap()
    kvp = dout("kvp", [DEPTH, 2048, 1024]).ap()
    kvs = dout("kvs", [DEPTH, 64, 1024]).ap()
    winp = dout("winp", [DEPTH, 512, 512]).ap()
    wins = dout("wins", [DEPTH, 16, 512, 512]).ap()
    convp = dout("convp", [DEPTH, 30, 512]).ap()
    convs = dout("convs", [DEPTH, 16, 30, 512]).ap()
    dbg_out = {}

    with ExitStack() as st:
        arena = st.enter_context(nc.sbuf_tensor("arena", [128, 104000], BF16))
        PSB = [st.enter_context(nc.psum_tensor("ps%d" % i, [128, 512], F32)) for i in range(8)]

        def ps(i):
            return PSB[i][:]

        def psk(i):
            return ("ps", i)

        def mm(out, lhsT, rhs, start, stop, R, W, signal=True, skip=False):
            P.op("pe", lambda e: e.matmul(out, lhsT=lhsT, rhs=rhs, start=start, stop=stop, skip_group_check=skip),
                 reads=R, writes=W, signal=signal)

        def tr(out, in_, ident, R, W, signal=True):
            P.op("pe", lambda e: e.transpose(out=out, in_=in_, identity=ident), reads=R, writes=W, signal=signal)

        def act(out, in_, func, R, W, scale=None, bias=None, accum=None):
            kw = {}
            if scale is not None:
                kw["scale"] = scale
            if bias is not None:
                kw["bias"] = bias
            if accum is not None:
                kw["accum_out"] = accum
            P.op("act", lambda e: e.activation(out=out, in_=in_, func=func, **kw), reads=R, writes=W)

        def tt(eng, out, in0, in1, op, R, W):
            P.op(eng, lambda e: e.tensor_tensor(out=out, in0=in0, in1=in1, op=op), reads=R, writes=W)

        def tsc(eng, out, in0, s1, op0, R, W, s2=None, op1=None):
            if op1 is None:
                P.op(eng, lambda e: e.tensor_scalar(out=out, in0=in0, scalar1=s1, scalar2=None, op0=op0), reads=R, writes=W)
            else:
                P.op(eng, lambda e: e.tensor_scalar(out=out, in0=in0, scalar1=s1, scalar2=s2, op0=op0, op1=op1),
                     reads=R, writes=W)

        def stt(out, in0, scalar, in1, op0, op1, R, W):
            P.op("dve", lambda e: e.scalar_tensor_tensor(out=out, in0=in0, scalar=scalar, in1=in1, op0=op0, op1=op1),
                 reads=R, writes=W)

        def cp(eng, out, in_, R, W):
            if eng == "act":
                P.op("act", lambda e: e.copy(out=out, in_=in_), reads=R, writes=W)
            else:
                P.op(eng, lambda e: e.tensor_copy(out=out, in_=in_), reads=R, writes=W)

        def recip(out, in_, R, W):
            P.op("dve", lambda e: e.reciprocal(out=out, in_=in_), reads=R, writes=W)

        def memset(eng, ap, val, W):
            P.op(eng, lambda e: e.memset(ap, val), writes=W)

        def ld(out, in_, W, R=(), stream="in"):
            P.dma("sp", out, in_, reads=R, writes=W, stream=stream)

        def ldc(out, in_, W, R=(), stream="w"):
            P.dma("pool", out, in_, reads=R, writes=W, stream=stream)

        def stv(out, in_, R, W=(), stream="out"):
            P.dma("sp", out, in_, reads=R, writes=W, stream=stream)

        def dump(name, ap, shape, dt=F32):
            if name in dbg_names:
                t = dout("dbg_" + name, shape, dt).ap()
                dbg_out[name] = t
                P.barrier()
                P.dma("sp", t, ap, stream="out")
                P.barrier()

        LP = Lay(arena, 0, 28672)
        IDB = LP.take([128], BF16)
        IDF = LP.take([128], F32)
        TRILE = LP.take([128], BF16)
        TRIGT = LP.take([128], BF16)
        MSTX = LP.take([2048], BF16)
        CMPMASK = LP.take([2048], BF16, parts=64)
        VCAC = LP.take([34], BF16, parts=64)
        SELBIAS = LP.take([16, 32], F32)
        SELBIAS_S = LP.take([33], F32, parts=4)
        SELP = LP.take([128], F32, parts=17)
        SELS = LP.take([64], F32, parts=17)
        RSM = LP.take([4, 4], F32, parts=16)
        RTM = LP.take([4], F32, parts=16)
        WMASK = LP.take([4], BF16)
        ONESF = LP.take([128], F32)
        IOTA = LP.take([1], I32)
        SCT = LP.take([8, 32], BF16)
        ADAT = LP.take([48, 17], F32)
        GNT = LP.take([8, 4], F32)
        A_M = LP.take([8, 17], F32)
        A_F = LP.take([8, 17], F32)
        ADA_GT = LP.take([2, 1024], F32, parts=17)
        SS = LP.take([8], F32)
        PIDX = LP.take([256], I32)
        HT = Lay(arena, 28672, 62464).take([8, TT], BF16)
        OFF_YC = 62464
        OFF_DYN = 79360
        OFF_END = 208000
        YC = Lay(arena, OFF_YC, OFF_DYN).take([4, TT], BF16)

        for (dst, src, key) in [(IDB, inp("k_identb"), "idb"), (IDF, inp("k_identf"), "idf"), (TRILE, inp("k_trile"), "trile"),
                                (TRIGT, inp("k_trigt"), "trigt"), (MSTX, inp("k_mstx"), "mstx"), (CMPMASK, inp("k_cmpmask"), "cmpmask"),
                                (VCAC, inp("k_vcac"), "vcac"), (SELBIAS, inp("k_selbias").rearrange("p (a b) -> p a b", b=32), "selbias"),
                                (SELBIAS_S, inp("k_selbias_s"), "selbias_s"), (SELP, inp("k_selp"), "selp"), (SELS, inp("k_sels"), "sels"),
                                (RSM, inp("k_rs").rearrange("p (a b) -> p a b", b=4), "rsm"), (RTM, inp("k_rt"), "rtm"),
                                (WMASK, inp("k_wmask"), "wmask"), (ONESF, inp("k_onesf"), "onesf"), (IOTA, inp("k_iota"), "iota")]:
            ld(dst, src, W=[key])
        P.barrier()

        def norm_to_T(L, j, xt, xt_key, A, SH, DST, dst_key, bank, dcol=None):
            c0, n = tcols(j)
            if dcol is not None:
                c0 = dcol
            SQ = L["SQ"]
            XN = L["XN"]
            ssv = SS[0:n, 0:1]
            rsv = SS[0:n, 1:2]
            act(SQ[0:n, :], xt[0:n, :], AF.Square, R=[xt_key], W=["SQ", "ss0"], accum=ssv)
            act(rsv, ssv, AF.Sqrt, R=["ss0"], W=["ss1"], scale=1.0 / D, bias=EPS)
            recip(rsv, rsv, R=["ss1"], W=["ss1"])
            tsc("dve", XN[0:n, :], xt[0:n, :], rsv, ALU.mult, R=[xt_key, "ss1"], W=["XN"])
            pT = ps(bank).bitcast(BF16)
            for k in range(8):
                tr(pT[:, k * 128:k * 128 + n], XN[0:n, k * 128:(k + 1) * 128], IDB[0:n, 0:n], R=["XN", "idb"],
                   W=[psk(bank)], signal=(k == 7))
            if j < 16:
                for k in range(8):
                    if k % 2 == 0:
                        act(DST[:, k, c0:c0 + 128], pT[:, k * 128:(k + 1) * 128], AF.Identity, R=[psk(bank)],
                            W=[(dst_key, j)], scale=A[:, k, 0:1], bias=SH[:, k, 0:1])
                    else:
                        tsc("dve", DST[:, k, c0:c0 + 128], pT[:, k * 128:(k + 1) * 128], A[:, k, 0:1], ALU.mult,
                            R=[psk(bank)], W=[(dst_key, j)], s2=SH[:, k, 0:1], op1=ALU.add)
            else:
                TMPS = L["TMPS"]
                for k in range(8):
                    src = pT[:, k * 128:k * 128 + 64].rearrange("p (b t) -> p b t", t=4)
                    tt("dve", TMPS[:, :, :], src, bc_last(A[:, k, 1:17], 4), ALU.mult, R=[psk(bank)], W=["TMPS"])
                    tt("dve", DST[:, k, c0:c0 + 64].rearrange("p (b t) -> p b t", t=4), TMPS[:, :, :],
                       bc_last(SH[:, k, 1:17], 4), ALU.add, R=["TMPS"], W=[(dst_key, j)])

        def load_w(dst, src2d, col0, ncols, W, R=(), krows=128):
            K = dst.shape[1]
            for c in range(0, ncols, 512):
                cw = min(512, ncols - c)
                ldc(dst[:, :, c:c + cw], src2d[:, col0 + c:col0 + c + cw].rearrange("(k p) n -> p k n", p=krows),
                    W=W, R=R)

        for l in range(DEPTH):
            xsrc_p = inp("xp") if l == 0 else yp
            xsrc_s = inp("xs") if l == 0 else ys

            def xrows(j):
                c0, n = tcols(j)
                return (xsrc_p[c0:c0 + n, :] if j < 16 else xsrc_s[:, :])

            def yrows(j):
                c0, n = tcols(j)
                return (yp[c0:c0 + n, :] if j < 16 else ys[:, :])

            LD_ = Lay(arena, 28672, OFF_END)
            C17 = LD_.take([1024], F32, parts=32)
            SC17 = LD_.take([1024], BF16, parts=32)
            WB = [LD_.take([8, 512], BF16) for _ in range(2)]
            BADA = LD_.take([6144], F32, parts=17)
            ADA_TM = LD_.take([6144], F32, parts=17)
            GN = LD_.take([1024], F32, parts=4)
            G1B = LD_.take([2, 1024], F32, parts=17)
            if l == 0:
                memset("dve", C17, 0.0, W=["c17"])
                ld(C17[0:17, :], inp("c17"), W=["c17"], R=["c17"])
                act(SC17, C17, AF.Silu, R=["c17"], W=["sc17"])
                pT = ps(0).bitcast(BF16)
                for k in range(8):
                    tr(pT[:, k * 32:k * 32 + 32], SC17[:, k * 128:(k + 1) * 128], IDB[0:32, 0:32], R=["sc17", "idb"],
                       W=[psk(0)], signal=(k == 7))
                cp("dve", SCT, pT[:, 0:256].rearrange("p (k c) -> p k c", c=32), R=[psk(0)], W=["sct"])
            ld(BADA, inp("b_ada")[l:l + 1, :].partition_broadcast(17).rearrange("p a n -> p (a n)"), W=["bada"])
            ld(GN, inp("gain")[l], W=["gn"])
            ld(G1B[:, 0, :], inp("gain")[l, 1:2, :].partition_broadcast(17).rearrange("p a n -> p (a n)"), W=["g1b"])
            ld(G1B[:, 1, :], inp("gain")[l, 3:4, :].partition_broadcast(17).rearrange("p a n -> p (a n)"), W=["g1b"])
            for ng in range(12):
                wb = WB[ng % 2]
                wk = ("wb", ng % 2)
                load_w(wb, inp("w_ada")[l], ng * 512, 512, W=[wk])
                bank = 1 + ng % 2
                for k in range(8):
                    mm(ps(bank)[0:32, :], SCT[:, k, :], wb[:, k, :], k == 0, k == 7, R=["sct", wk], W=[psk(bank)],
                       signal=(k == 7))
                tt("dve", ADA_TM[:, ng * 512:(ng + 1) * 512], ps(bank)[0:17, :], BADA[:, ng * 512:(ng + 1) * 512], ALU.add,
                   R=[psk(bank), "bada"], W=["ada_tm"])
            for half in range(2):
                bank = 3 + half
                for c in range(24):
                    cc = half * 24 + c
                    tr(ps(bank)[:, c * 20:c * 20 + 17], ADA_TM[:, cc * 128:(cc + 1) * 128], IDF[0:17, 0:17],
                       R=["ada_tm", "idf"], W=[psk(bank)], signal=(c == 23))
                cp("dve", ADAT[:, half * 24:(half + 1) * 24, :],
                   ps(bank)[:, 0:480].rearrange("p (c k) -> p c k", k=20)[:, :, 0:17], R=[psk(bank)], W=["adat"])
            for k in range(8):
                tr(ps(5)[:, k * 4:k * 4 + 4], GN[:, k * 128:(k + 1) * 128], IDF[0:4, 0:4], R=["gn", "idf"], W=[psk(5)],
                   signal=(k == 7))
            cp("dve", GNT, ps(5)[:, 0:32].rearrange("p (k c) -> p k c", c=4), R=[psk(5)], W=["gnt"])
            for k in range(8):
                tsc("dve", A_M[:, k, :], ADAT[:, 8 + k, :], 1.0, ALU.add, R=["adat", "gnt"], W=["a_m"],
                    s2=GNT[:, k, 0:1], op1=ALU.mult)
                tsc("dve", A_F[:, k, :], ADAT[:, 32 + k, :], 1.0, ALU.add, R=["adat", "gnt"], W=["a_f"],
                    s2=GNT[:, k, 2:3], op1=ALU.mult)
            SH_M = ADAT[:, 0:8, :]
            SH_F = ADAT[:, 24:32, :]
            tt("dve", ADA_GT[:, 0, :], ADA_TM[:, 2048:3072], G1B[:, 0, :], ALU.mult, R=["ada_tm", "g1b"], W=["ada_gt"])
            tt("dve", ADA_GT[:, 1, :], ADA_TM[:, 5120:6144], G1B[:, 1, :], ALU.mult, R=["ada_tm", "g1b"], W=["ada_gt"])
            P.barrier()
            dump("ada_tm%d" % l, ADA_TM, [17, 6144])
            dump("a_m%d" % l, A_M, [128, 8 * 17])
            if STOP == "ada":
                break

            LD_ = Lay(arena, OFF_DYN, OFF_END)
            LN = {"SQ": LD_.take([1024], BF16), "XN": LD_.take([1024], BF16), "TMPS": LD_.take([16, 4], F32)}
            XT = [LD_.take([1024], F32) for _ in range(2)]
            for j in range(NT):
                c0, n = tcols(j)
                xt = XT[j % 2]
                ld(xt[0:n, :], xrows(j), W=[("xt", j % 2)], R=[("yrow", j)])
                norm_to_T(LN, j, xt, ("xt", j % 2), A_M, SH_M, HT, "ht", bank=j % 2)
            P.barrier()
            dump("ht%d" % l, HT, [128, 8 * TT], BF16)
            if STOP == "norm0":
                break
            w_in_l = inp("w_in")[l]
            LQ = Lay(arena, 114176, OFF_END)
            QT = LQ.take([8, TT], BF16)
            G3 = LQ.take([NT, 48], F32)
            KST = LQ.take([2, TT], BF16)
            KWT = LQ.take([2, TT], BF16)
            VSA = LQ.take([16, 4, 66], BF16)
            VWA = LQ.take([16, 4, 66], BF16)
            KCC = LQ.take([2, 64], BF16)
            VCA = LQ.take([4, 98], BF16, parts=64)
            off_rest = LQ.off
            WBQ = [LQ.take([8, 512], BF16) for _ in range(2)]
            LB = Lay(arena, OFF_YC, OFF_DYN)
            KCT = LB.take([2, TT], BF16)
            VCT = LB.take([2, TT], BF16)
            LO = Lay(arena, OFF_DYN, 114176)
            STG = [LO.take([512], F32) for _ in range(2)]
            W1 = LO.take([2, 32, 128], BF16)
            W2K = LO.take([64], BF16)
            W2V = LO.take([64], BF16)
            B1 = LO.take([2], F32)
            HIDF = LO.take([256], F32)
            HIDG = LO.take([256], F32)
            HID = [LO.take([256], BF16) for _ in range(2)]
            WG48 = LO.take([8, 48], BF16)

            memset("dve", VSA, 0.0, W=["vsa"])
            memset("dve", VWA, 0.0, W=["vwa"])
            memset("dve", VSA[:, :, :, 64:65], 1.0, W=["vsa"])
            memset("dve", VWA[:, :, :, 64:65], 1.0, W=["vwa"])
            TCH = [(0, 512), (512, 512), (1024, 512), (1536, 512), (2048, 64)]
            wbi = [0]

            def next_wb(col0, ncols=512):
                i = wbi[0] % 2
                wbi[0] += 1
                load_w(WBQ[i][:, :, 0:ncols], w_in_l, col0, ncols, W=[("wbq", i)])
                return WBQ[i], ("wbq", i)

            evi = [0]

            def evac(out, in_, R, W, scale=None):
                evi[0] += 1
                if evi[0] % 2 == 0:
                    if scale is None:
                        cp("act", out, in_, R=R, W=W)
                    else:
                        act(out, in_, AF.Identity, R=R, W=W, scale=scale)
                else:
                    if scale is None:
                        cp("dve", out, in_, R=R, W=W)
                    else:
                        tsc("dve", out, in_, scale, ALU.mult, R=R, W=W)

            bki = [0]

            def nbank(lo=0, n=4):
                bki[0] += 1
                return lo + bki[0] % n

            for p_ in range(2):
                wi = wbi[0] % 2
                wbi[0] += 1
                wb, wk = WBQ[wi], ("wbq", wi)
                for i_ in range(4):
                    for h_ in range(2):
                        cs = p_ * 512 + h_ * 256 + i_ * 64
                        ldc(wb[:, :, i_ * 128 + h_ * 64:i_ * 128 + h_ * 64 + 64],
                            w_in_l[:, cs:cs + 64].rearrange("(k p) n -> p k n", p=128), W=[wk])
                for i_ in range(4):
                    ci = 4 * p_ + i_
                    for (t0, n) in TCH:
                        b = nbank()
                        for k in range(8):
                            mm(ps(b)[:, 0:n], wb[:, k, i_ * 128:(i_ + 1) * 128], HT[:, k, t0:t0 + n], k == 0, k == 7, R=[wk, "ht"],
                               W=[psk(b)], signal=(k == 7))
                        evac(QT[:, ci, t0:t0 + n], ps(b)[:, 0:n], R=[psk(b)], W=["qt"], scale=ATTN_SCALE)
            if STOP == "proj_q":
                P.barrier()
                break
            for grp in range(3):
                wb, wk = next_wb(1024 + grp * 512)
                if grp == 0:
                    fm = [(KCT, 0, 0, "kct"), (KCT, 1, 128, "kct"), (VCT, 0, 256, "vct"), (VCT, 1, 384, "vct")]
                elif grp == 1:
                    fm = [(KST, 0, 0, "kst"), (KST, 1, 128, "kst")]
                else:
                    fm = [(KWT, 0, 0, "kwt"), (KWT, 1, 128, "kwt")]
                for (dst, ch, cof, key) in fm:
                    for (t0, n) in TCH:
                        b = nbank()
                        for k in range(8):
                            mm(ps(b)[:, 0:n], wb[:, k, cof:cof + 128], HT[:, k, t0:t0 + n], k == 0, k == 7, R=[wk, "ht"],
                               W=[psk(b)], signal=(k == 7))
                        evac(dst[:, ch, t0:t0 + n], ps(b)[:, 0:n], R=[psk(b)], W=[key])
                for j in range(NT):
                    c0, n = tcols(j)
                    b = nbank()
                    for k in range(8):
                        mm(ps(b)[0:n, :], HT[:, k, c0:c0 + n], wb[:, k, :], k == 0, k == 7, R=[wk, "ht"], W=[psk(b)],
                           signal=(k == 7))
                    sg = STG[j % 2]
                    sgk = ("stg", j % 2)
                    cp("act", sg[0:n, :], ps(b)[0:n, :], R=[psk(b)], W=[sgk])
                    if j < 16:
                        if grp < 2:
                            stv(kvp[l, c0:c0 + n, grp * 512:(grp + 1) * 512], sg[0:n, :], R=[sgk])
                        elif j >= 12:
                            stv(winp[l, c0 - 1536:c0 - 1536 + n, :], sg[0:n, :], R=[sgk])
                        if grp == 1:
                            cp("dve", VSA[:, j, :, 0:64], sg[:, 256:512].rearrange("p (g d) -> p g d", d=64),
                               R=[sgk], W=["vsa"])
                        if grp == 2:
                            cp("dve", VWA[:, j, :, 0:64], sg[:, 256:512].rearrange("p (g d) -> p g d", d=64),
                               R=[sgk], W=["vwa"])
                    else:
                        if grp < 2:
                            stv(kvs[l, :, grp * 512:(grp + 1) * 512], sg[0:64, :], R=[sgk])
                        else:
                            for bb in range(16):
                                stv(wins[l, bb, 508:512, :], sg[4 * bb:4 * bb + 4, :], R=[sgk])
            if STOP == "proj_kv":
                P.barrier()
                break
            ldc(WG48, w_in_l[:, 2560:2608].rearrange("(k p) n -> p k n", p=128), W=["wg48"])
            for j in range(NT):
                c0, n = tcols(j)
                b = nbank()
                for k in range(8):
                    mm(ps(b)[0:n, 0:48], HT[:, k, c0:c0 + n], WG48[:, k, :], k == 0, k == 7, R=["wg48", "ht"], W=[psk(b)],
                       signal=(k == 7))
                act(G3[0:n, j, :], ps(b)[0:n, 0:48], AF.Sigmoid, R=[psk(b)], W=["g3"])
            if STOP == "proj_g":
                P.barrier()
                break
            w1v = inp("w_cmp1")[l]
            for kv in range(2):
                for dup in range(2):
                    ldc(W1[64 * dup:64 * dup + 64, kv, :, :], w1v[kv].rearrange("(pos hd) n -> hd pos n", hd=64),
                        W=["w1"])
            ldc(W2K, inp("w_cmp2")[l, 0], W=["w2k"])
            ldc(W2V, inp("w_cmp2")[l, 1], W=["w2v"])
            P.dma("sp", B1, inp("b_cmp1")[l].rearrange("k n -> n k"), writes=["b1"], allow_slow_non_contiguous=True)

            if STOP == "proj_w":
                P.barrier()
                break

            def compress(SRC_K, SRC_V, kkey, vkey, tok0, KCCd, VCAd, tagk, tagv, nblk=64):
                for kv, SRC, skey in ((0, SRC_K, kkey), (1, SRC_V, vkey)):
                    for par in range(2):
                        b = 4 + par
                        ph = 64 * par
                        first = True
                        for g in (par, par + 2):
                            for pos in range(32):
                                last = (g == par + 2 and pos == 31)
                                mm(ps(b)[:, (g // 2) * 64:(g // 2) * 64 + nblk], W1[ph:ph + 64, kv, pos, :],
                                   SRC[ph:ph + 64, g // 2, tok0 + pos:tok0 + nblk * 32:32], first, last, R=["w1", skey],
                                   W=[psk(b)], signal=last, skip=True)
                                first = False
                    HIDFv = HIDF.rearrange("p (a q b) -> p a q b", a=2, q=2, b=64)
                    for par in range(2):
                        act(HIDFv[:, :, par, :], ps(4 + par)[:, 0:128].rearrange("p (a b) -> p a b", b=64), AF.Identity,
                            R=[psk(4 + par), "b1"], W=["hidf"], bias=B1[:, kv:kv + 1])
                    tt("dve", HIDG, HIDF, HIDF, ALU.mult, R=["hidf"], W=["hidg"])
                    tsc("dve", HIDG, HIDG, 0.044715, ALU.mult, R=["hidg"], W=["hidg"], s2=1.0, op1=ALU.add)
                    tt("dve", HIDG, HIDG, HIDF, ALU.mult, R=["hidg", "hidf"], W=["hidg"])
                    act(HIDG, HIDG, AF.Sigmoid, R=["hidg"], W=["hidg"], scale=1.5957691216057308)
                    tt("dve", HID[kv], HIDG, HIDF, ALU.mult, R=["hidg", "hidf"], W=[("hid", kv)])
                for p_ in range(2):
                    for half in range(2):
                        g = 2 * p_ + half
                        mm(ps(6)[64 * half:64 * half + 64, p_ * 64:p_ * 64 + nblk], W2K, HID[0][:, g * 64:g * 64 + nblk],
                           True, True, R=["w2k", ("hid", 0)], W=[psk(6)], signal=(g == 3))
                cp("dve", KCCd[:, :, 0:nblk], ps(6)[:, 0:128].rearrange("p (a b) -> p a b", b=64)[:, :, 0:nblk],
                   R=[psk(6)], W=[tagk])
                for g in range(4):
                    mm(ps(7)[0:nblk, g * 64:(g + 1) * 64], HID[1][:, g * 64:g * 64 + nblk], W2V, True, True,
                       R=["w2v", ("hid", 1)], W=[psk(7)], signal=(g == 3))
                cp("dve", VCAd[0:nblk, :, 0:64], ps(7)[0:nblk, 0:256].rearrange("p (g d) -> p g d", d=64), R=[psk(7)],
                   W=[tagv])

            for g in range(4):
                cp("dve", VCA[:, g, 64:98], VCAC, R=["vcac"], W=["vca"])
            compress(KCT, VCT, "kct", "vct", 0, KCC, VCA, "kcc", "vca")
            P.barrier()
            dump("qt%d" % l, QT, [128, 8 * TT], BF16)
            dump("kcc%d" % l, KCC, [128, 128], BF16)
            dump("vca%d" % l, VCA, [64, 4 * 98], BF16)
            dump("g3%d" % l, G3, [128, NT * 48])
            if STOP == "proj":
                break

            LA = Lay(arena, off_rest, OFF_END)
            EB = [LA.take([512], BF16) for _ in range(3)]
            SELB = LA.take([4, 32], BF16)
            SELBT = LA.take([4, 128], BF16)
            LA2 = Lay(arena, OFF_YC, OFF_DYN)
            OACC = LA2.take([16, 64], F32)
            TMPO = LA2.take([4, 64], F32)
            PSL = LA2.take([4, 32], F32)
            SCR = LA2.take([4, 32], F32)
            SC2 = LA2.take([32], F32)
            M1 = LA2.take([8], F32)
            M2 = LA2.take([8], F32)
            RD = LA2.take([4], F32)
            WGT = LA2.take([4], F32)
            ONSA = Lay(arena, OFF_DYN, 114176).take([NT, 1024], BF16)
            memset("dve", SELBT, 0.0, W=["selbt"])
            sbi = [0]
            ebi = [0]

            def g3col(G, j, g, br):
                return G[:, j, :].rearrange("p (h b) -> p h b", b=3)[:, 4 * g:4 * g + 4, br]

            def post(accb, ncol, g, br, G3v, nq, first, pslc=False):
                accv = ps(accb)[0:nq, 0:4 * ncol].rearrange("p (h c) -> p h c", c=ncol)
                tsc("dve", RD[0:nq, :], accv[:, :, 64], 1e-30, ALU.max, R=[psk(accb)], W=["rd"])
                recip(RD[0:nq, :], RD[0:nq, :], R=["rd"], W=["rd"])
                if pslc:
                    tsc("dve", PSL[0:nq, g, :], accv[:, 0, 65:97], RD[0:nq, 0:1], ALU.mult, R=[psk(accb), "rd"], W=["psl"])
                    for h in range(1, 4):
                        stt(PSL[0:nq, g, :], accv[:, h, 65:97], RD[0:nq, h:h + 1], PSL[0:nq, g, :], ALU.mult, ALU.add,
                            R=[psk(accb), "rd", "psl"], W=["psl"])
                tt("dve", WGT[0:nq, :], RD[0:nq, :], G3v, ALU.mult, R=["rd", "g3", "g3s"], W=["wgt"])
                if first:
                    tt("dve", OACC[0:nq, 4 * g:4 * g + 4, :], accv[:, :, 0:64], bc_last(WGT[0:nq, :], 64), ALU.mult,
                       R=[psk(accb), "wgt"], W=["oacc"])
                else:
                    tt("dve", TMPO[0:nq, :, :], accv[:, :, 0:64], bc_last(WGT[0:nq, :], 64), ALU.mult,
                       R=[psk(accb), "wgt"], W=["tmpo"])
                    tt("dve", OACC[0:nq, 4 * g:4 * g + 4, :], OACC[0:nq, 4 * g:4 * g + 4, :], TMPO[0:nq, :, :], ALU.add,
                       R=["tmpo", "oacc"], W=["oacc"])

            for j in range(16):
                c0 = j * 128
                nb = min(64, 4 * j + 4)
                for g in range(4):
                    p_, ph = g // 2, 64 * (g % 2)
                    Q = QT[ph:ph + 64, 4 * p_:4 * p_ + 4, c0:c0 + 128]
                    sb = 2 * (g % 2) + sbi[0] % 2
                    sbi[0] += 1
                    mm(ps(sb)[0:nb, :], KCC[ph:ph + 64, p_, 0:nb], Q, True, True, R=["kcc", "qt"], W=[psk(sb)])
                    E = EB[ebi[0] % 3]
                    ek = ("eb", ebi[0] % 3)
                    ebi[0] += 1
                    act(E[0:nb, :], ps(sb)[0:nb, :], AF.Exp, R=[psk(sb)], W=[ek])
                    tt("dve", E[0:nb, :].rearrange("p (h t) -> p h t", t=128), E[0:nb, :].rearrange("p (h t) -> p h t", t=128),
                       bc_mid(CMPMASK[0:nb, c0:c0 + 128], 4), ALU.mult, R=[ek, "cmpmask"], W=[ek])
                    ab = 4
                    for h in range(4):
                        mm(ps(ab)[:, h * 98:(h + 1) * 98], E[0:nb, h * 128:(h + 1) * 128], VCA[0:nb, g, :], h == 0, h == 3,
                           R=[ek, "vca"], W=[psk(ab)], signal=(h == 3), skip=True)
                    post(ab, 98, g, 0, g3col(G3, j, g, 0), 128, True, pslc=(j >= 8))
                if j >= 8:
                    tt("dve", SCR, PSL, bc_mid(SELBIAS[:, j, :], 4), ALU.add, R=["psl", "selbias"], W=["scr"])
                    sb = 7
                    pT = ps(sb).bitcast(BF16)
                    for g in range(4):
                        P.op("dve", lambda e, g=g: e.max(out=M1, in_=SCR[:, g, :]), reads=["scr"], writes=["m1"])
                        P.op("dve", lambda e, g=g: e.match_replace(out=SC2, in_to_replace=M1, in_values=SCR[:, g, :],
                                                                    imm_value=-3.0e9), reads=["scr", "m1"], writes=["sc2"])
                        P.op("dve", lambda e: e.max(out=M2, in_=SC2), reads=["sc2"], writes=["m2"])
                        tsc("dve", SELB[:, g, :], SCR[:, g, :], M2[:, 7:8], ALU.is_lt, R=["scr", "m2"], W=["selb"],
                            s2=-30000.0, op1=ALU.mult)
                        ph = 64 * (g % 2)
                        tr(pT[ph:ph + 32, g * 128:(g + 1) * 128], SELB[:, g, :], IDB, R=["selb", "idb"], W=[psk(sb)])
                        cp("act", SELBT[ph:ph + 32, g, :], pT[ph:ph + 32, g * 128:(g + 1) * 128], R=[psk(sb)], W=["selbt"])
                for br, KT_, VA_, kkey, vkey, kb0 in ((1, KST, VSA, "kst", "vsa", 0), (2, KWT, VWA, "kwt", "vwa", max(0, j - 4))):
                    for g in range(4):
                        p_, ph = g // 2, 64 * (g % 2)
                        Q = QT[ph:ph + 64, 4 * p_:4 * p_ + 4, c0:c0 + 128]
                        ab = 4 + br
                        def qk_p(kb):
                            sb = 2 * (g % 2) + sbi[0] % 2
                            sbi[0] += 1
                            bias = (br == 1 and j >= 8)
                            if bias:
                                mm(ps(sb), MSTX[ph:ph + 64, kb * 128:(kb + 1) * 128], bc_mid(SELBT[ph:ph + 64, g, :], 4),
                                   True, False, R=["mstx", "selbt"], W=[psk(sb)], signal=False)
                            mm(ps(sb), KT_[ph:ph + 64, p_, kb * 128:(kb + 1) * 128], Q, not bias, True, R=[kkey, "qt"],
                               W=[psk(sb)])
                            return sb

                        def rest_p(kb, sb):
                            E = EB[ebi[0] % 3]
                            ek = ("eb", ebi[0] % 3)
                            ebi[0] += 1
                            act(E, ps(sb), AF.Exp, R=[psk(sb)], W=[ek])
                            msk = None
                            if kb == j:
                                msk = TRILE
                            elif br == 2 and kb == j - 4:
                                msk = TRIGT
                            if msk is not None:
                                tt("dve", E.rearrange("p (h t) -> p h t", t=128), E.rearrange("p (h t) -> p h t", t=128),
                                   bc_mid(msk, 4), ALU.mult, R=[ek, "trile", "trigt"], W=[ek])
                            for h in range(4):
                                mm(ps(ab)[:, h * 66:(h + 1) * 66], E[:, h * 128:(h + 1) * 128], VA_[:, kb, g, :],
                                   kb == kb0 and h == 0, kb == j and h == 3, R=[ek, vkey], W=[psk(ab)], signal=(h == 3),
                                   skip=True)

                        prev = None
                        for kb in range(kb0, j + 1):
                            sb_ = qk_p(kb)
                            if prev is not None:
                                rest_p(*prev)
                            prev = (kb, sb_)
                        rest_p(*prev)
                        post(ab, 66, g, br, g3col(G3, j, g, br), 128, False)
                cp("act", ONSA[:, j, :], OACC.rearrange("p h d -> p (h d)"), R=["oacc"], W=[("onsa", j)])
            memset("dve", ONSA[:, 16, :], 0.0, W=[("onsa", 16)])
            P.barrier()
            dump("onsa%d" % l, ONSA, [128, NT * 1024], BF16)
            if STOP == "attn":
                break
            if not early:
                NROWS = DEPTH * NPOOL_ROWS
                cache2d = inp("cache")
                LS = Lay(arena, 114176, OFF_END)
                QTS = LS.take([8, 64], BF16)
                KSN = LS.take([2, 64], BF16)
                KWN = LS.take([2, 64], BF16)
                WVN = LS.take([8, 512], BF16)
                WG48b = LS.take([8, 48], BF16)
                PG = LS.take([8, 1024], BF16)
                FMb = LS.take([3, 2, 2048], BF16)
                VSAb = LS.take([16, 4, 66], BF16)
                SWb = LS.take([4, 512], BF16)
                KWTb = LS.take([2, 512], BF16)
                VWAb = LS.take([4, 4, 66], BF16)
                KCCb = LS.take([2, 64], BF16)
                VCAb = LS.take([4, 98], BF16, parts=64)
                VNEW = LS.take([2, 4, 66], BF16, parts=4)
                G3b = LS.take([48], F32, parts=4)
                ESB = [LS.take([2, 32], BF16) for _ in range(3)]
                SELB4 = LS.take([4, 32], BF16, parts=4)
                SELBT4 = LS.take([4, 4], BF16)
                SCR4 = LS.take([4, 34], F32, parts=4)
                SC24 = LS.take([34], F32, parts=4)
                M14 = LS.take([8], F32, parts=4)
                M24 = LS.take([8], F32, parts=4)
                RD16 = LS.take([4], F32, parts=16)
                ON16 = LS.take([4, 64], F32, parts=16)
                PS16 = LS.take([4, 32], F32, parts=16)
                OACC4 = LS.take([16, 64], F32, parts=4)
                OB4 = LS.take([1024], BF16, parts=4)
                TMP4 = LS.take([4, 64], F32, parts=4)
                PTF = LS.take([256], F32)
                IOTAF = LS.take([1], F32)
                HIDF = LS.take([256], F32)
                HIDG = LS.take([256], F32)
                HID = [LS.take([256], BF16) for _ in range(2)]
                W2K = LS.take([64], BF16)
                W2V = LS.take([64], BF16)
                B1 = LS.take([2], F32)
                W1 = Lay(arena, OFF_YC, OFF_DYN).take([2, 32, 128], BF16)
                cp("dve", QTS, QT[:, :, 2048:2112], R=["qt"], W=["qts"])
                cp("dve", KSN, KST[:, :, 2048:2112], R=["kst"], W=["ksn"])
                cp("dve", KWN, KWT[:, :, 2048:2112], R=["kwt"], W=["kwn"])
                P.barrier()
                ldc(WVN[:, :, 0:256], w_in_l[:, 1792:2048].rearrange("(k p) n -> p k n", p=128), W=["wvn"])
                ldc(WVN[:, :, 256:512], w_in_l[:, 2304:2560].rearrange("(k p) n -> p k n", p=128), W=["wvn"])
                ldc(WG48b, w_in_l[:, 2560:2608].rearrange("(k p) n -> p k n", p=128), W=["wg48b"])
                for kv in range(2):
                    for dup in range(2):
                        ldc(W1[64 * dup:64 * dup + 64, kv, :, :], w1v[kv].rearrange("(pos hd) n -> hd pos n", hd=64),
                            W=["w1"])
                ldc(W2K, inp("w_cmp2")[l, 0], W=["w2k"])
                ldc(W2V, inp("w_cmp2")[l, 1], W=["w2v"])
                P.dma("sp", B1, inp("b_cmp1")[l].rearrange("k n -> n k"), writes=["b1"], allow_slow_non_contiguous=True)
                ld(PIDX, inp("pt").rearrange("b g -> (b g)").partition_broadcast(128), W=["pidx"])
                cp("dve", PTF, PIDX, R=["pidx"], W=["ptf"])
                cp("dve", IOTAF, IOTA, R=["iota"], W=["iotaf"])
                tsc("dve", PTF, PTF, 128.0, ALU.mult, R=["ptf", "iotaf"], W=["ptf"], s2=IOTAF[:, 0:1], op1=ALU.add)
                tsc("dve", PTF, PTF, float(l * NPOOL_ROWS), ALU.add, R=["ptf"], W=["ptf"])
                cp("dve", PIDX, PTF, R=["ptf"], W=["pidx"])
                memset("dve", VSAb, 0.0, W=["vsab"])
                memset("dve", VSAb[:, :, :, 64:65], 1.0, W=["vsab"])
                memset("dve", VWAb, 0.0, W=["vwab"])
                memset("dve", VWAb[:, :, :, 64:65], 1.0, W=["vwab"])
                memset("dve", VNEW, 0.0, W=["vnew"])
                memset("dve", VNEW[:, :, :, 64:65], 1.0, W=["vnew"])
                memset("dve", SELBT4, 0.0, W=["selbt4"])
                memset("dve", SCR4, 1.0e4, W=["scr4"])
                for g in range(4):
                    cp("dve", VCAb[:, g, 64:98], VCAC, R=["vcac"], W=["vcab"])
                stv(wins[l, :, 0:508, :], inp("swin")[l, :, 4:512, :], R=[])
                esi = [0]
                for bb in range(16):
                    tb = 4 * bb
                    def gather(pg):
                        col = bb * 16 + pg
                        P.dma("pool", PG[:, pg % 8, :], cache2d[:, :], reads=["pidx"], writes=[("pg", pg % 8)],
                              indirect=bass.IndirectOffsetOnAxis(ap=PIDX[:, col:col + 1], axis=0),
                              bounds_check=NROWS - 1, oob_is_err=False)

                    for pg in range(8):
                        gather(pg)
                    ldc(SWb, inp("swin")[l, bb].rearrange("(q p) c -> p q c", p=128), W=["swb"])
                    for pg in range(16):
                        bk = 6 + pg % 2
                        pT = ps(bk).bitcast(BF16)
                        for t3 in range(3):
                            for c in range(2):
                                i6 = t3 * 2 + c
                                tr(pT[:, i6 * 128:(i6 + 1) * 128], PG[:, pg % 8, t3 * 256 + c * 128:t3 * 256 + (c + 1) * 128], IDB,
                                   R=[("pg", pg % 8), "idb"], W=[psk(bk)], signal=(i6 == 5))
                        evac(FMb[:, :, :, pg * 128:(pg + 1) * 128].rearrange("p a c t -> p (a c) t"),
                             pT[:, 0:768].rearrange("p (s t) -> p s t", t=128), R=[psk(bk)], W=["fmb"])
                        cp("dve", VSAb[:, pg, :, 0:64], PG[:, pg % 8, 768:1024].rearrange("p (g d) -> p g d", d=64),
                           R=[("pg", pg % 8)], W=["vsab"])
                        if pg + 8 < 16:
                            gather(pg + 8)
                    for q4 in range(4):
                        bk = 6 + q4 % 2
                        pT = ps(bk).bitcast(BF16)
                        for c in range(2):
                            tr(pT[:, c * 128:(c + 1) * 128], SWb[:, q4, c * 128:(c + 1) * 128], IDB, R=["swb", "idb"],
                               W=[psk(bk)], signal=(c == 1))
                        evac(KWTb[:, :, q4 * 128:(q4 + 1) * 128], pT[:, 0:256].rearrange("p (s t) -> p s t", t=128),
                             R=[psk(bk)], W=["kwtb"])
                        cp("dve", VWAb[:, q4, :, 0:64], SWb[:, q4, 256:512].rearrange("p (g d) -> p g d", d=64), R=["swb"],
                           W=["vwab"])
                    compress(FMb[:, 0], FMb[:, 1], "fmb", "fmb", 0, KCCb, VCAb, "kccb", "vcab")
                    for s2 in range(2):
                        for k in range(8):
                            mm(ps(6)[0:4, s2 * 256:(s2 + 1) * 256], HT[:, k, 2048 + tb:2048 + tb + 4],
                               WVN[:, k, s2 * 256:(s2 + 1) * 256], k == 0, k == 7, R=["ht", "wvn"], W=[psk(6)],
                               signal=(k == 7))
                    cp("dve", VNEW[:, :, :, 0:64], ps(6)[0:4, :].rearrange("p (s g d) -> p s g d", s=2, d=64), R=[psk(6)],
                       W=["vnew"])
                    for k in range(8):
                        mm(ps(7)[0:4, 0:48], HT[:, k, 2048 + tb:2048 + tb + 4], WG48b[:, k, :], k == 0, k == 7,
                           R=["ht", "wg48b"], W=[psk(7)], signal=(k == 7))
                    act(G3b, ps(7)[0:4, 0:48], AF.Sigmoid, R=[psk(7)], W=["g3b"])
                    G3bv = G3b.rearrange("p (g h b) -> p g h b", h=4, b=3)

                    def Qs(g):
                        p_, ph = g // 2, 64 * (g % 2)
                        return QTS[ph:ph + 64, 4 * p_:4 * p_ + 4, tb:tb + 4]

                    def post_s(accb, ncol, br, first, pslc=False):
                        accv = ps(accb)[0:16, 0:4 * ncol].rearrange("p (g c) -> p g c", c=ncol)
                        tsc("dve", RD16, accv[:, :, 64], 1e-30, ALU.max, R=[psk(accb)], W=["rd16"])
                        recip(RD16, RD16, R=["rd16"], W=["rd16"])
                        tt("dve", ON16, accv[:, :, 0:64], bc_last(RD16, 64), ALU.mult, R=[psk(accb), "rd16"], W=["on16"])
                        if pslc:
                            tt("dve", PS16, accv[:, :, 65:97], bc_last(RD16, 32), ALU.mult, R=[psk(accb), "rd16"], W=["ps16"])
                            mm(ps(7)[0:4, 256:384], RTM, PS16.rearrange("p g s -> p (g s)"), True, True, R=["rtm", "ps16"],
                               W=[psk(7)])
                        for h in range(4):
                            mm(ps(7)[0:4, 0:256], RSM[:, h, :], ON16.rearrange("p g d -> p (g d)"), True, True,
                               R=["rsm", "on16"], W=[psk(7)])
                            src = ps(7)[0:4, 0:256].rearrange("p (g d) -> p g d", d=64)
                            dst = OACC4.rearrange("p (g h) d -> p g h d", h=4)[:, :, h, :]
                            gate = bc_last(G3bv[:, :, h, br], 64)
                            if first:
                                tt("dve", dst, src, gate, ALU.mult, R=[psk(7), "g3b"], W=["oacc4"])
                            else:
                                tt("dve", TMP4, src, gate, ALU.mult, R=[psk(7), "g3b"], W=["tmp4"])
                                tt("dve", dst, dst, TMP4, ALU.add, R=["tmp4", "oacc4"], W=["oacc4"])

                    def exp_pair(nk, alt=0):
                        E = ESB[esi[0] % 3]
                        ek = ("esb", esi[0] % 3)
                        esi[0] += 1
                        for par in range(2):
                            act(E[0:nk, par, :], ps(2 * par + alt)[0:nk, 0:32], AF.Exp, R=[psk(2 * par + alt)], W=[ek])
                        return E, ek

                    for g in range(4):
                        p_, ph = g // 2, 64 * (g % 2)
                        mm(ps(2 * (g % 2))[0:64, (g // 2) * 16:(g // 2) * 16 + 16], KCCb[ph:ph + 64, p_, :], Qs(g), g < 2, g >= 2,
                           R=["kccb", "qts"], W=[psk(2 * (g % 2))], skip=True)
                    E, ek = exp_pair(64)
                    for g in range(4):
                        mm(ps(4)[0:16, g * 98:(g + 1) * 98], E[0:64, g % 2, (g // 2) * 16:(g // 2) * 16 + 16], VCAb[:, g, :],
                           g == 0, g == 3, R=[ek, "vcab"], W=[psk(4)], signal=(g == 3), skip=True)
                    post_s(4, 98, 0, True, pslc=True)
                    tt("dve", SCR4[:, :, 0:32], ps(7)[0:4, 256:384].rearrange("p (g s) -> p g s", s=32),
                       bc_mid(SELBIAS_S[:, 0:32], 4), ALU.add, R=[psk(7), "selbias_s"], W=["scr4"])
                    pT = ps(6).bitcast(BF16)
                    for g in range(4):
                        P.op("dve", lambda e, g=g: e.max(out=M14, in_=SCR4[:, g, 0:33]), reads=["scr4"], writes=["m14"])
                        P.op("dve", lambda e, g=g: e.match_replace(out=SC24[:, 0:33], in_to_replace=M14, in_values=SCR4[:, g, 0:33],
                                                                    imm_value=-3.0e9), reads=["scr4", "m14"], writes=["sc24"])
                        P.op("dve", lambda e: e.max(out=M24, in_=SC24[:, 0:33]), reads=["sc24"], writes=["m24"])
                        tsc("dve", SELB4[:, g, :], SCR4[:, g, 0:32], M24[:, 7:8], ALU.is_lt, R=["scr4", "m24"], W=["selb4"],
                            s2=-30000.0, op1=ALU.mult)
                        ph = 64 * (g % 2)
                        tr(pT[ph:ph + 32, g * 4:g * 4 + 4], SELB4[:, g, :], IDB[0:4, 0:4], R=["selb4", "idb"], W=[psk(6)])
                        cp("act", SELBT4[ph:ph + 32, g, :], pT[ph:ph + 32, g * 4:g * 4 + 4], R=[psk(6)], W=["selbt4"])
                    for br in (1, 2):
                        nkb = 16 if br == 1 else 4
                        def qk_s(kb):
                            new = (kb == nkb)
                            nk = 4 if new else 128
                            alt = kb % 2
                            for g in range(4):
                                p_, ph = g // 2, 64 * (g % 2)
                                sbk = 2 * (g % 2) + alt
                                out = ps(sbk)[0:nk, (g // 2) * 16:(g // 2) * 16 + 16]
                                if br == 1:
                                    lhs = KSN[ph:ph + 64, p_, tb:tb + 4] if new else FMb[ph:ph + 64, 2, p_, kb * 128:(kb + 1) * 128]
                                else:
                                    lhs = KWN[ph:ph + 64, p_, tb:tb + 4] if new else KWTb[ph:ph + 64, p_, kb * 128:(kb + 1) * 128]
                                bias = (br == 1 and not new)
                                if bias:
                                    mm(out, MSTX[ph:ph + 64, kb * 128:(kb + 1) * 128], bc_mid(SELBT4[ph:ph + 64, g, :], 4),
                                       g < 2, False, R=["mstx", "selbt4"], W=[psk(sbk)], signal=False, skip=True)
                                mm(out, lhs, Qs(g), (g < 2) and not bias, True, R=["ksn", "kwn", "fmb", "kwtb", "qts"],
                                   W=[psk(sbk)], skip=True)

                        def rest_s(kb):
                            new = (kb == nkb)
                            nk = 4 if new else 128
                            E, ek = exp_pair(nk, kb % 2)
                            Ev = E.rearrange("p a (q h t) -> p a q h t", q=2, h=4)
                            if new:
                                for par in range(2):
                                    for q2 in range(2):
                                        tt("dve", Ev[0:4, par, q2], Ev[0:4, par, q2], bc_mid(TRILE[0:4, 0:4], 4), ALU.mult,
                                           R=[ek, "trile"], W=[ek])
                            elif br == 2 and kb == 0:
                                for par in range(2):
                                    for q2 in range(2):
                                        tt("dve", Ev[:, par, q2], Ev[:, par, q2], bc_mid(WMASK, 4), ALU.mult, R=[ek, "wmask"],
                                           W=[ek])
                            ab = 4 + br
                            for g in range(4):
                                if new:
                                    rhs = VNEW[:, br - 1, g, :]
                                elif br == 1:
                                    rhs = VSAb[:, kb, g, :]
                                else:
                                    rhs = VWAb[:, kb, g, :]
                                mm(ps(ab)[0:16, g * 66:(g + 1) * 66], E[0:nk, g % 2, (g // 2) * 16:(g // 2) * 16 + 16], rhs,
                                   kb == 0 and g == 0, new and g == 3, R=[ek, "vnew", "vsab", "vwab"], W=[psk(ab)],
                                   signal=(g == 3), skip=True)

                        for kb in range(nkb + 1):
                            qk_s(kb)
                            if kb > 0:
                                rest_s(kb - 1)
                        rest_s(nkb)
                        post_s(4 + br, 66, br, False)
                    cp("act", OB4, OACC4.rearrange("p h d -> p (h d)"), R=["oacc4"], W=["ob4"])
                    P.dma("sp", ONSA[tb:tb + 4, 16, :], OB4, reads=["ob4"], writes=[("onsa", 16)])
                P.barrier()
                dump("onsas%d" % l, ONSA[0:64, 16, :], [64, 1024], BF16)
            if STOP == "attns":
                break
            LC = Lay(arena, 114176, OFF_END)
            WGL = LC.take([8, 1024], BF16)
            CIN = LC.take([4, 2078], BF16)
            CINS = LC.take([4, 16, 34], BF16)
            ACC = LC.take([4, 1024], F32)
            ACCS = LC.take([4, 64], F32)
            CVT = LC.take([4, 34], F32)
            CV34 = LC.take([512], F32, parts=34)
            MEAN = LC.take([512], F32)
            RSTD = LC.take([512], F32)
            SQT = [LC.take([512], F32) for _ in range(2)]
            ZT = LC.take([512], F32)
            SIGB = [LC.take([512], F32) for _ in range(2)]
            GLS = LC.take([512], F32)
            SCS = LC.take([4, 512], F32, parts=120)
            load_w(WGL, w_in_l, 2608, 1024, W=["wgl"])
            ld(CV34, inp("cvp")[l], W=["cv34"])
            for c in range(4):
                tr(ps(0)[:, c * 34:(c + 1) * 34], CV34[:, c * 128:(c + 1) * 128], IDF[0:34, 0:34], R=["cv34", "idf"],
                   W=[psk(0)], signal=(c == 3))
            cp("dve", CVT, ps(0)[:, 0:136].rearrange("p (c k) -> p c k", k=34), R=[psk(0)], W=["cvt"])
            memset("dve", CIN[:, :, 0:30], 0.0, W=["cin"])
            for c in range(4):
                for (t0, n) in TCH:
                    ba = nbank(0, 2)
                    bb_ = 2 + ba
                    for k in range(8):
                        mm(ps(ba)[:, 0:n], WGL[:, k, c * 128:(c + 1) * 128], HT[:, k, t0:t0 + n], k == 0, k == 7,
                           R=["wgl", "ht"], W=[psk(ba)], signal=(k == 7))
                    for k in range(8):
                        mm(ps(bb_)[:, 0:n], WGL[:, k, 512 + c * 128:512 + (c + 1) * 128], HT[:, k, t0:t0 + n], k == 0, k == 7,
                           R=["wgl", "ht"], W=[psk(bb_)], signal=(k == 7))
                    sg = SIGB[ba]
                    act(sg[:, 0:n], ps(bb_)[:, 0:n], AF.Sigmoid, R=[psk(bb_)], W=[("sigb", ba)])
                    if t0 < 2048:
                        tt("dve", CIN[:, c, 30 + t0:30 + t0 + n], ps(ba)[:, 0:n], sg[:, 0:n], ALU.mult,
                           R=[psk(ba), ("sigb", ba)], W=["cin"])
                    else:
                        tt("dve", CINS[:, c, :, 30:34], ps(ba)[:, 0:64].rearrange("p (b t) -> p b t", t=4),
                           sg[:, 0:64].rearrange("p (b t) -> p b t", t=4), ALU.mult, R=[psk(ba), ("sigb", ba)], W=["cins"])
            for j in (15, 16):
                c0, n = tcols(j)
                for k in range(8):
                    mm(ps(4)[0:n, :], HT[:, k, c0:c0 + n], WGL[:, k, 0:512], k == 0, k == 7, R=["wgl", "ht"], W=[psk(4)],
                       signal=(k == 7))
                for k in range(8):
                    mm(ps(5)[0:n, :], HT[:, k, c0:c0 + n], WGL[:, k, 512:1024], k == 0, k == 7, R=["wgl", "ht"], W=[psk(5)],
                       signal=(k == 7))
                act(SIGB[0][0:n, :], ps(5)[0:n, :], AF.Sigmoid, R=[psk(5)], W=[("sigb", 0)])
                tt("dve", GLS[0:n, :], ps(4)[0:n, :], SIGB[0][0:n, :], ALU.mult, R=[psk(4), ("sigb", 0)], W=["gls"])
                if j == 15:
                    stv(convp[l], GLS[98:128, :], R=["gls"])
                else:
                    for bb in range(16):
                        stv(convs[l, bb, 26:30, :], GLS[4 * bb:4 * bb + 4, :], R=["gls"])
            if not early:
                scv = inp("sconv")[l].rearrange("(b r) c -> b r c", r=30)
                stv(convs[l, :, 0:26, :], scv[:, 4:30, :], R=[])
                ld(SCS, inp("sconv")[l].rearrange("(q r) c -> r q c", r=120), W=["scs"])
                for q in range(4):
                    for c in range(4):
                        b = nbank(0, 4)
                        tr(ps(b)[:, 0:120], SCS[:, q, c * 128:(c + 1) * 128], IDF[0:120, 0:120], R=["scs", "idf"],
                           W=[psk(b)])
                        cp("dve", CINS[:, c, 4 * q:4 * q + 4, 0:30], ps(b)[:, 0:120].rearrange("p (b r) -> p b r", r=30),
                           R=[psk(b)], W=["cins"])
            else:
                memset("dve", CINS[:, :, :, 0:30], 0.0, W=["cins"])

            def layer_norm_chunk(ACCv, n, t0):
                for c in range(4):
                    mm(ps(4)[:, 0:n], ONESF, ACCv[:, c, :], c == 0, c == 3, R=["onesf", "acc"], W=[psk(4)], signal=(c == 3))
                for c in range(4):
                    act(SQT[c % 2][:, 0:n], ACCv[:, c, :], AF.Square, R=["acc"], W=[("sqt", c % 2)])
                    mm(ps(5)[:, 0:n], ONESF, SQT[c % 2][:, 0:n], c == 0, c == 3, R=["onesf", ("sqt", c % 2)], W=[psk(5)])
                act(MEAN[:, 0:n], ps(4)[:, 0:n], AF.Identity, R=[psk(4)], W=["mean"], scale=1.0 / 512)
                tt("dve", ZT[:, 0:n], MEAN[:, 0:n], MEAN[:, 0:n], ALU.mult, R=["mean"], W=["zt"])
                stt(RSTD[:, 0:n], ps(5)[:, 0:n], 1.0 / 512, ZT[:, 0:n], ALU.mult, ALU.subtract, R=[psk(5), "zt"], W=["rstd"])
                act(RSTD[:, 0:n], RSTD[:, 0:n], AF.Sqrt, R=["rstd"], W=["rstd"], bias=EPS)
                recip(RSTD[:, 0:n], RSTD[:, 0:n], R=["rstd"], W=["rstd"])
                for c in range(4):
                    tt("dve", ZT[:, 0:n], ACCv[:, c, :], MEAN[:, 0:n], ALU.subtract, R=["acc", "mean"], W=["zt"])
                    tt("dve", ZT[:, 0:n], ZT[:, 0:n], RSTD[:, 0:n], ALU.mult, R=["zt", "rstd"], W=["zt"])
                    tsc("dve", ZT[:, 0:n], ZT[:, 0:n], CVT[:, c, 32:33], ALU.mult, R=["zt", "cvt"], W=["zt"],
                        s2=CVT[:, c, 33:34], op1=ALU.add)
                    act(YC[:, c, t0:t0 + n], ZT[:, 0:n], AF.Silu, R=["zt"], W=["yc"])

            for half in range(2):
                h0 = half * 1024
                for c in range(4):
                    tsc("dve", ACC[:, c, :], CIN[:, c, h0:h0 + 1024], CVT[:, c, 0:1], ALU.mult, R=["cin", "cvt"], W=["acc"],
                        s2=CVT[:, c, 31:32], op1=ALU.add)
                    for k in range(1, 31):
                        stt(ACC[:, c, :], CIN[:, c, h0 + k:h0 + k + 1024], CVT[:, c, k:k + 1], ACC[:, c, :], ALU.mult, ALU.add,
                            R=["cin", "cvt", "acc"], W=["acc"])
                for q in range(2):
                    layer_norm_chunk(ACC[:, :, q * 512:(q + 1) * 512], 512, h0 + q * 512)
            ACCSv = ACCS.rearrange("p c (b t) -> p c b t", t=4)
            for c in range(4):
                tsc("dve", ACCSv[:, c, :, :], CINS[:, c, :, 0:4], CVT[:, c, 0:1], ALU.mult, R=["cins", "cvt"], W=["acc"],
                    s2=CVT[:, c, 31:32], op1=ALU.add)
                for k in range(1, 31):
                    stt(ACCSv[:, c, :, :], CINS[:, c, :, k:k + 4], CVT[:, c, k:k + 1], ACCSv[:, c, :, :], ALU.mult, ALU.add,
                        R=["cins", "cvt", "acc"], W=["acc"])
            layer_norm_chunk(ACCS, 64, 2048)
            P.barrier()
            dump("yc%d" % l, YC, [128, 4 * TT], BF16)
            if STOP == "conv":
                break

            LM = Lay(arena, 114176, OFF_END)
            WMG = LM.take([8, 2048], BF16)
            WPW = LM.take([4, 1024], BF16)
            WOUT = LM.take([8, 1024], BF16)
            GATE = LM.take([1024], F32)
            T1 = LM.take([1024], F32)
            MB = LM.take([1024], BF16)
            MT = LM.take([8, 128], BF16)
            XT = [LM.take([1024], F32) for _ in range(2)]
            YN = LM.take([1024], F32)
            SQJ = LM.take([1024], BF16)
            load_w(WMG, w_in_l, 3632, 2048, W=["wmg"])
            load_w(WPW, inp("w_pw2")[l], 0, 1024, W=["wpw"])
            load_w(WOUT, inp("w_out")[l], 0, 1024, W=["wout"])

            def residual(j, n, yb0, yb1, gb0, gb1, gi, xt, xk, SQJ_, YN_):
                sel = SELP if j < 16 else SELS
                act(SQJ_[0:n, 0:512], ps(yb0)[0:n, :], AF.Square, R=[psk(yb0)], W=["sqj", "ssa"], accum=SS[0:n, 2:3])
                act(SQJ_[0:n, 512:1024], ps(yb1)[0:n, :], AF.Square, R=[psk(yb1)], W=["sqj", "ssb"], accum=SS[0:n, 3:4])
                tt("dve", SS[0:n, 4:5], SS[0:n, 2:3], SS[0:n, 3:4], ALU.add, R=["ssa", "ssb"], W=["ssc"])
                act(SS[0:n, 4:5], SS[0:n, 4:5], AF.Sqrt, R=["ssc"], W=["ssc"], scale=1.0 / D, bias=EPS)
                recip(SS[0:n, 4:5], SS[0:n, 4:5], R=["ssc"], W=["ssc"])
                for hh, (yb, gb) in enumerate(((yb0, gb0), (yb1, gb1))):
                    mm(ps(gb)[0:n, :], sel[:, 0:n], ADA_GT[:, gi, hh * 512:(hh + 1) * 512], True, True,
                       R=["selp", "sels", "ada_gt"], W=[psk(gb)])
                    act(YN_[0:n, hh * 512:(hh + 1) * 512], ps(yb)[0:n, :], AF.Identity, R=[psk(yb), "ssc"], W=["yn"],
                        scale=SS[0:n, 4:5])
                    tt("dve", YN_[0:n, hh * 512:(hh + 1) * 512], YN_[0:n, hh * 512:(hh + 1) * 512], ps(gb)[0:n, :], ALU.mult,
                       R=["yn", psk(gb)], W=["yn"])
                tt("dve", xt[0:n, :], xt[0:n, :], YN_[0:n, :], ALU.add, R=[xk, "yn"], W=[xk])
                stv(yrows(j), xt[0:n, :], R=[xk], W=[("yrow", j)])

            for j in range(NT):
                c0, n = tcols(j)
                xt = XT[j % 2]
                xk = ("xt", j % 2)
                ld(xt[0:n, :], xrows(j), W=[xk])
                for hh in range(2):
                    for k in range(8):
                        mm(ps(hh)[0:n, :], HT[:, k, c0:c0 + n], WMG[:, k, hh * 512:(hh + 1) * 512], k == 0, k == 7,
                           R=["ht", "wmg"], W=[psk(hh)], signal=(k == 7))
                    act(GATE[0:n, hh * 512:(hh + 1) * 512], ps(hh)[0:n, :], AF.Sigmoid, R=[psk(hh)], W=["gate"])
                tt("dve", T1[0:n, :], GATE[0:n, :], ONSA[0:n, j, :], ALU.mult, R=["gate", ("onsa", j)], W=["t1"])
                for hh in range(2):
                    for k in range(8):
                        mm(ps(hh)[0:n, :], HT[:, k, c0:c0 + n], WMG[:, k, 1024 + hh * 512:1024 + (hh + 1) * 512], k == 0,
                           k == 7, R=["ht", "wmg"], W=[psk(hh)], signal=(k == 7))
                    act(GATE[0:n, hh * 512:(hh + 1) * 512], ps(hh)[0:n, :], AF.Sigmoid, R=[psk(hh)], W=["gate"])
                    for c in range(4):
                        mm(ps(2 + hh)[0:n, :], YC[:, c, c0:c0 + n], WPW[:, c, hh * 512:(hh + 1) * 512], c == 0, c == 3,
                           R=["yc", "wpw"], W=[psk(2 + hh)], signal=(c == 3))
                    tt("dve", GATE[0:n, hh * 512:(hh + 1) * 512], GATE[0:n, hh * 512:(hh + 1) * 512], ps(2 + hh)[0:n, :],
                       ALU.mult, R=["gate", psk(2 + hh)], W=["gate"])
                tt("dve", MB[0:n, :], T1[0:n, :], GATE[0:n, :], ALU.add, R=["t1", "gate"], W=["mb"])
                pT = ps(4).bitcast(BF16)
                for k in range(8):
                    tr(pT[:, k * 128:k * 128 + n], MB[0:n, k * 128:(k + 1) * 128], IDB[0:n, 0:n], R=["mb", "idb"], W=[psk(4)],
                       signal=(k == 7))
                cp("act", MT[:, :, 0:n], pT.rearrange("p (k t) -> p k t", t=128)[:, :, 0:n], R=[psk(4)], W=["mt"])
                for hh in range(2):
                    for k in range(8):
                        mm(ps(5 + hh)[0:n, :], MT[:, k, 0:n], WOUT[:, k, hh * 512:(hh + 1) * 512], k == 0, k == 7,
                           R=["mt", "wout"], W=[psk(5 + hh)], signal=(k == 7))
                residual(j, n, 5, 6, 7, 3, 0, xt, xk, SQJ, YN)
            P.barrier()
            if STOP == "merge":
                break

            LF = Lay(arena, 28672, OFF_END)
            WO = LF.take([NFC, 1024], BF16)
            H2T = LF.take([8, 512], BF16)
            UT = LF.take([NFC, 512], BF16)
            WA = [LF.take([8, 256], BF16) for _ in range(2)]
            WB_ = [LF.take([8, 256], BF16) for _ in range(2)]
            XC = LF.take([4, 1024], F32)
            LNF = {"SQ": LF.take([1024], BF16), "XN": LF.take([1024], BF16), "TMPS": LF.take([16, 4], F32)}
            SIL = [LF.take([512], F32) for _ in range(2)]
            YNF = LF.take([1024], F32)
            SQF = LF.take([1024], BF16)
            w_f1_l = inp("w_f1")[l]
            for c in range(0, 1024, 512):
                ldc(WO[:, :, c:c + 512], inp("w_f2")[l][:, c:c + 512].rearrange("(k p) n -> p k n", p=128), W=["wo"])
            for tc in range(5):
                tiles = [4 * tc + i for i in range(4)] if tc < 4 else [16]
                ntok = 512 if tc < 4 else 64
                for i, j in enumerate(tiles):
                    c0, n = tcols(j)
                    ld(XC[0:n, i, :], yrows(j), W=[("xc", i)], R=[("yrow", j)])
                    norm_to_T(LNF, j, XC[:, i, :], ("xc", i), A_F, SH_F, H2T, "h2t", bank=6 + i % 2, dcol=i * 128)
                for fg in range(11):
                    wa = WA[fg % 2]
                    wb2 = WB_[fg % 2]
                    load_w(wa, w_f1_l, fg * 256, 256, W=[("wa", fg % 2)])
                    load_w(wb2, w_f1_l, D_FFN + fg * 256, 256, W=[("wb_", fg % 2)])
                    for f2 in range(2):
                        fc = 2 * fg + f2
                        ba = f2
                        bb_ = 2 + f2
                        for k in range(8):
                            mm(ps(ba)[:, 0:ntok], wa[:, k, f2 * 128:(f2 + 1) * 128], H2T[:, k, 0:ntok], k == 0, k == 7,
                               R=[("wa", fg % 2)] + [("h2t", jj) for jj in tiles], W=[psk(ba)], signal=(k == 7))
                        for k in range(8):
                            mm(ps(bb_)[:, 0:ntok], wb2[:, k, f2 * 128:(f2 + 1) * 128], H2T[:, k, 0:ntok], k == 0, k == 7,
                               R=[("wb_", fg % 2)] + [("h2t", jj) for jj in tiles], W=[psk(bb_)], signal=(k == 7))
                        act(SIL[f2][:, 0:ntok], ps(ba)[:, 0:ntok], AF.Silu, R=[psk(ba)], W=[("sil", f2)])
                        tt("dve", UT[:, fc, 0:ntok], SIL[f2][:, 0:ntok], ps(bb_)[:, 0:ntok], ALU.mult,
                           R=[("sil", f2), psk(bb_)], W=["ut"])
                for i, j in enumerate(tiles):
                    c0, n = tcols(j)
                    for hh in range(2):
                        for fc in range(NFC):
                            mm(ps(4 + hh)[0:n, :], UT[:, fc, i * 128:i * 128 + n], WO[:, fc, hh * 512:(hh + 1) * 512], fc == 0,
                               fc == NFC - 1, R=["ut", "wo"], W=[psk(4 + hh)], signal=(fc == NFC - 1))
                    residual(j, n, 4, 5, 6, 7, 1, XC[:, i, :], ("xc", i), SQF, YNF)
            P.barrier()

        P.barrier()
        P.emit(st)
    return nc, dbg_out


def make_consts():
    bf = ml_dtypes.bfloat16
    c = {}
    c["k_identb"] = np.eye(128, dtype=np.float32).astype(bf)
    c["k_identf"] = np.eye(128, dtype=np.float32)
    p = np.arange(128)[:, None]
    f = np.arange(128)[None, :]
    c["k_trile"] = (p <= f).astype(np.float32).astype(bf)
    c["k_trigt"] = (p > f).astype(np.float32).astype(bf)
    m = np.zeros((128, 2048), np.float32)
    r = np.arange(32)[:, None]
    cc = np.arange(2048)[None, :]
    mst2 = (cc // 64 == r).astype(np.float32)
    m[0:32] = mst2
    m[64:96] = mst2
    c["k_mstx"] = m.astype(bf)
    i = np.arange(64)[:, None]
    c["k_cmpmask"] = (cc >= 32 * i + 31).astype(np.float32).astype(bf)
    v = np.zeros((64, 34), np.float32)
    v[:, 0] = 1.0
    v[:, 1:33] = (np.arange(64)[:, None] // 2 == np.arange(32)[None, :])
    c["k_vcac"] = v.astype(bf)
    sb = np.zeros((128, 16, 32), np.float32)
    for j in range(16):
        for pp in range(128):
            cur = (128 * j + pp) // 64
            s = np.arange(32)
            forced = (s == 0) | (s == cur) | (s == cur - 1)
            valid = s <= cur
            sb[pp, j] = np.where(valid, np.where(forced, 1e4, 0.0), -1e9)
    c["k_selbias"] = sb.reshape(128, 512)
    ss = np.zeros((4, 33), np.float32)
    ss[:, [0, 31, 32]] = 1e4
    c["k_selbias_s"] = ss
    sp = np.zeros((17, 128), np.float32)
    sp[0] = 1.0
    c["k_selp"] = sp
    s_ = np.zeros((17, 64), np.float32)
    for b in range(16):
        s_[1 + b, 4 * b:4 * b + 4] = 1.0
    c["k_sels"] = s_
    rs = np.zeros((16, 4, 4), np.float32)
    rt = np.zeros((16, 4), np.float32)
    for h in range(4):
        for t in range(4):
            rs[h * 4 + t, h, t] = 1.0
            rt[h * 4 + t, t] = 1.0
    c["k_rs"] = rs.reshape(16, 16)
    c["k_rt"] = rt
    c["k_wmask"] = (np.arange(128)[:, None] >= np.arange(4)[None, :] + 1).astype(np.float32).astype(bf)
    c["k_onesf"] = np.ones((128, 128), np.float32)
    c["k_iota"] = np.arange(128, dtype=np.int32).reshape(128, 1)
    return c


def make_in_maps(inp):
    consts = make_consts()
    f32 = lambda a: np.ascontiguousarray(a, dtype=np.float32)
    cache = inp["cache_kv"]
    if cache.size == DEPTH * NPOOL_ROWS * 1024:
        cache = f32(cache).reshape(DEPTH * NPOOL_ROWS, 1024)
    cvp = np.ascontiguousarray(np.concatenate([inp["w_dw"], inp["b_dw"][:, None, :], inp["ln_conv_g"][:, None, :],
                                               inp["ln_conv_b"][:, None, :]], axis=1), dtype=np.float32)
    shared = {
        "cache": cache, "w_ada": f32(inp["w_ada"]), "b_ada": f32(inp["b_ada"]), "gain": f32(inp["norm_gain"]),
        "w_in": f32(inp["w_in"]), "w_cmp1": f32(inp["w_cmp1"]), "b_cmp1": f32(inp["b_cmp1"]), "w_cmp2": f32(inp["w_cmp2"]),
        "cvp": cvp, "w_pw2": f32(inp["w_pw2"]), "w_out": f32(inp["w_out"]), "w_f1": f32(inp["w_ffn_in"]),
        "w_f2": f32(inp["w_ffn_out"]),
    }
    shared.update(consts)
    maps = []
    for i in range(8):
        b0 = 16 * i
        m = dict(shared)
        m["xp"] = f32(inp["x_prompt"][i])
        m["xs"] = f32(inp["x_sample"][b0:b0 + 16]).reshape(64, D)
        m["c17"] = np.ascontiguousarray(np.concatenate([inp["c_prompt"][i:i + 1], inp["c_sample"][b0:b0 + 16]], 0), np.float32)
        m["pt"] = np.ascontiguousarray(inp["page_table"][b0:b0 + 16], dtype=np.int32)
        m["swin"] = f32(inp["state_win"][:, b0:b0 + 16]).reshape(DEPTH, 16, 512, 512)
        m["sconv"] = f32(inp["state_conv"][:, b0:b0 + 16]).reshape(DEPTH, 480, 512)
        maps.append(m)
    return maps


_NC_CACHE = {}


def kernel(**inputs):
    if "nc" not in _NC_CACHE:
        _NC_CACHE["nc"] = build_program()
    nc, _ = _NC_CACHE["nc"]
    maps = make_in_maps(inputs)
    used = set(a.memorylocations[0].name for a in nc.allocations
               if isinstance(a, mybir.MemoryLocationSet) and a.kind == "ExternalInput")
    maps = [{k: v for k, v in m.items() if k in used} for m in maps]
    res = run_bass_kernel_spmd(nc, maps, core_ids=list(range(8))).results
    y_p = np.stack([r["yp"] for r in res], 0)
    y_s = np.concatenate([r["ys"].reshape(16, 4, D) for r in res], 0)
    kv_p = np.stack([r["kvp"].reshape(DEPTH, 2048, 4, 4, 64) for r in res], 1)
    kv_s = np.concatenate([r["kvs"].reshape(DEPTH, 16, 4, 4, 4, 64) for r in res], 1)
    win_p = np.stack([r["winp"].reshape(DEPTH, 512, 2, 4, 64) for r in res], 1)
    win_s = np.concatenate([r["wins"].reshape(DEPTH, 16, 512, 2, 4, 64) for r in res], 1)
    conv_p = np.stack([r["convp"] for r in res], 1)
    conv_s = np.concatenate([r["convs"] for r in res], 1)
    return (y_p, y_s, kv_p, kv_s, win_p, win_s, conv_p, conv_s)
```
